# Optimizing a Trainium2 kernel written in Bass

```python
import jax, jax.numpy as jnp
from jax import lax
import numpy as np

D_MODEL = 1024
BATCH = 2
SEQ = 8192
DEPTH = 2

N_MEM = 256
D_A = D_MODEL // 2
D_B = D_MODEL // 2
A_KERNEL = 31
B_KERNEL = 3
D_EVEN_IN = 2 * D_A + 3 * D_B
CHUNK = 128
C_GROUPS = 8
D_C = D_MODEL
C_GROUP_DIM = D_C // C_GROUPS
XA_HEADS = 4
XA_HEAD_DIM = D_MODEL // XA_HEADS
D_FF = ((8 * D_MODEL // 3 + 255) // 256) * 256
N_EVEN = (DEPTH + 1) // 2
N_ODD = DEPTH // 2
RMS_EPS = 1e-6
LN_EPS = 1e-5

kernel_name = 'hybrid_conv_sgu_memxattn_encoder'


def rms_norm(x, g):
    xf = x.astype(jnp.float32)
    y = xf * lax.rsqrt(jnp.mean(xf * xf, axis=-1, keepdims=True) + RMS_EPS)
    return (y * g.astype(jnp.float32)).astype(x.dtype)


def layer_norm(x, g, b):
    xf = x.astype(jnp.float32)
    mu = jnp.mean(xf, axis=-1, keepdims=True)
    var = jnp.mean(jnp.square(xf - mu), axis=-1, keepdims=True)
    y = (xf - mu) * lax.rsqrt(var + LN_EPS)
    return (y * g.astype(jnp.float32) + b.astype(jnp.float32)).astype(x.dtype)


def depthwise_conv(x, w, b):
    k = w.shape[0]
    pad = k // 2
    y = lax.conv_general_dilated(
        x, w[:, None, :].astype(x.dtype), window_strides=(1,),
        padding=[(pad, pad)], dimension_numbers=('NWC', 'WIO', 'NWC'),
        feature_group_count=x.shape[-1])
    return y + b


def conv_pair_mixer(n, w_in, a_conv_w, a_conv_b, a_ln_g, a_ln_b, b_conv_w, b_conv_b, w_out):
    z = n @ w_in
    a_val, a_gate, b_h, b_gb, b_gc = jnp.split(
        z, [D_A, 2 * D_A, 2 * D_A + D_B, 2 * D_A + 2 * D_B], axis=-1)
    a = a_val * jax.nn.sigmoid(a_gate)
    a = depthwise_conv(a, a_conv_w, a_conv_b)
    a = jax.nn.silu(layer_norm(a, a_ln_g, a_ln_b))
    b = b_gb * depthwise_conv(b_gc * b_h, b_conv_w, b_conv_b)
    return jnp.concatenate([a, b], axis=-1) @ w_out


def chunked_sgu_mixer(n, w_in, c_ln_g, c_ln_b, w_s, b_s, w_out):
    bsz, s, _ = n.shape
    z = jax.nn.gelu(n @ w_in)
    u, v = jnp.split(z, 2, axis=-1)
    v = layer_norm(v, c_ln_g, c_ln_b)
    v = v.reshape(bsz, s // CHUNK, CHUNK, C_GROUPS, C_GROUP_DIM)
    sv = jnp.einsum('gpq,bnqgc->bnpgc', w_s, v) + jnp.transpose(b_s)[:, :, None]
    y = u * sv.reshape(bsz, s, D_C)
    return y @ w_out


def memory_cross_attention(n, mem_n, w_q, w_k, w_v, w_o):
    bsz, s, _ = n.shape
    m = mem_n.shape[1]
    q = (n @ w_q).reshape(bsz, s, XA_HEADS, XA_HEAD_DIM)
    k = (mem_n @ w_k).reshape(bsz, m, XA_HEADS, XA_HEAD_DIM)
    v = (mem_n @ w_v).reshape(bsz, m, XA_HEADS, XA_HEAD_DIM)
    scores = jnp.einsum('bshd,bmhd->bhsm', q, k).astype(jnp.float32) * (XA_HEAD_DIM ** -0.5)
    p = jax.nn.softmax(scores, axis=-1).astype(v.dtype)
    o = jnp.einsum('bhsm,bmhd->bshd', p, v).reshape(bsz, s, D_MODEL)
    return o @ w_o


def swiglu(n, w_gate, w_up, w_down):
    return (jax.nn.silu(n @ w_gate) * (n @ w_up)) @ w_down


def setup_inputs(seed: int = 0) -> dict:
    key = jax.random.key(seed)
    ks = iter(jax.random.split(key, 40))

    def nrm(shape, scale):
        return jax.random.normal(next(ks), shape, jnp.float32) * scale

    def gain(shape):
        return 1.0 + 0.1 * jax.random.normal(next(ks), shape, jnp.float32)

    D = D_MODEL
    return {
        'x': nrm((BATCH, SEQ, D), 1.0),
        'mem': nrm((BATCH, N_MEM, D), 1.0),
        'g_mix': gain((DEPTH, D)),
        'g_xattn': gain((DEPTH, D)),
        'g_mem': gain((DEPTH, D)),
        'g_ffn': gain((DEPTH, D)),
        'g_final': gain((D,)),
        'ev_w_in': nrm((N_EVEN, D, D_EVEN_IN), D ** -0.5),
        'ev_a_conv_w': nrm((N_EVEN, A_KERNEL, D_A), A_KERNEL ** -0.5),
        'ev_a_conv_b': nrm((N_EVEN, D_A), 0.02),
        'ev_a_ln_g': gain((N_EVEN, D_A)),
        'ev_a_ln_b': nrm((N_EVEN, D_A), 0.02),
        'ev_b_conv_w': nrm((N_EVEN, B_KERNEL, D_B), B_KERNEL ** -0.5),
        'ev_b_conv_b': nrm((N_EVEN, D_B), 0.02),
        'ev_w_out': nrm((N_EVEN, D_A + D_B, D), (D_A + D_B) ** -0.5),
        'od_w_in': nrm((N_ODD, D, 2 * D_C), D ** -0.5),
        'od_c_ln_g': gain((N_ODD, D_C)),
        'od_c_ln_b': nrm((N_ODD, D_C), 0.02),
        'od_w_s': nrm((N_ODD, C_GROUPS, CHUNK, CHUNK), CHUNK ** -0.5),
        'od_b_s': gain((N_ODD, C_GROUPS, CHUNK)),
        'od_w_out': nrm((N_ODD, D_C, D), D_C ** -0.5),
        'xa_w_q': nrm((DEPTH, D, D), D ** -0.5),
        'xa_w_k': nrm((DEPTH, D, D), D ** -0.5),
        'xa_w_v': nrm((DEPTH, D, D), D ** -0.5),
        'xa_w_o': nrm((DEPTH, D, D), D ** -0.5),
        'ffn_w_gate': nrm((DEPTH, D, D_FF), D ** -0.5),
        'ffn_w_up': nrm((DEPTH, D, D_FF), D ** -0.5),
        'ffn_w_down': nrm((DEPTH, D_FF, D), D_FF ** -0.5),
    }


def reference(x, mem, g_mix, g_xattn, g_mem, g_ffn, g_final,
              ev_w_in, ev_a_conv_w, ev_a_conv_b, ev_a_ln_g, ev_a_ln_b,
              ev_b_conv_w, ev_b_conv_b, ev_w_out,
              od_w_in, od_c_ln_g, od_c_ln_b, od_w_s, od_b_s, od_w_out,
              xa_w_q, xa_w_k, xa_w_v, xa_w_o,
              ffn_w_gate, ffn_w_up, ffn_w_down):
    h = x
    for i in range(DEPTH):
        j = i // 2
        n = rms_norm(h, g_mix[i])
        if i % 2 == 0:
            h = h + conv_pair_mixer(n, ev_w_in[j], ev_a_conv_w[j], ev_a_conv_b[j],
                                    ev_a_ln_g[j], ev_a_ln_b[j], ev_b_conv_w[j],
                                    ev_b_conv_b[j], ev_w_out[j])
        else:
            h = h + chunked_sgu_mixer(n, od_w_in[j], od_c_ln_g[j], od_c_ln_b[j],
                                      od_w_s[j], od_b_s[j], od_w_out[j])
        mem_n = rms_norm(mem, g_mem[i])
        h = h + memory_cross_attention(rms_norm(h, g_xattn[i]), mem_n,
                                       xa_w_q[i], xa_w_k[i], xa_w_v[i], xa_w_o[i])
        h = h + swiglu(rms_norm(h, g_ffn[i]), ffn_w_gate[i], ffn_w_up[i], ffn_w_down[i])
    return rms_norm(h, g_final)
```

```python
import numpy as np
import concourse.bass as bass
import concourse.mybir as mybir
from concourse.bass_utils import run_bass_kernel_spmd

F32 = mybir.dt.float32
BF16 = mybir.dt.bfloat16
AF = mybir.ActivationFunctionType
ALU = mybir.AluOpType

NCORES = 8
D = 1024
KC = 8
SEQ = 8192
TCORE = 2048
TH = 1024
NHALF = 2
TT = 512
NTT = TH // TT
HALO = 16
XROWS = TH + 2 * HALO
DFF = 2816
FC = DFF // 128
NMEM = 256
NS = 4
WSLOT = 4096
RMS_EPS = 1e-6
LN_EPS = 1e-5
SB_BASE = 16512
SB_LIMIT = 229376

V_GMIX, V_GXA, V_GMEM, V_GFFN, V_GFIN = 0, 16, 32, 48, 64
V_CAW, V_CAB, V_LAG, V_LAB, V_CBW, V_CBB, V_LCG, V_LCB, NV = 72, 196, 200, 204, 208, 220, 224, 232, 240


class Slot:
    __slots__ = ("name", "w", "r")

    def __init__(self, name):
        self.name = name
        self.w = None
        self.r = {}


class Op:
    __slots__ = ("eng", "fn", "deps", "idx", "need_inc", "dma", "dma_val")


class Ring:
    def __init__(self, items):
        self.items = items
        self.i = 0

    def next(self):
        it = self.items[self.i % len(self.items)]
        self.i += 1
        return it


class Prog:
    ENGS = ("pe", "act", "dve", "pool", "sp")

    def __init__(self):
        self.streams = {e: [] for e in self.ENGS}
        self.dma_cnt = {}
        self.final_dma = set()

    def add(self, eng, fn, reads=(), writes=(), dma=None):
        o = Op()
        o.eng, o.fn, o.dma, o.need_inc, o.idx, o.dma_val = eng, fn, dma, False, None, None
        deps = {}
        for s in reads:
            if s.w is not None:
                deps[id(s.w)] = s.w
        for s in writes:
            if s.w is not None:
                deps[id(s.w)] = s.w
            for r in s.r.values():
                deps[id(r)] = r
        for s in reads:
            s.r[eng if dma is None else ("dma", id(o))] = o
        for s in writes:
            s.w = o
            s.r = {}
        deps.pop(id(o), None)
        o.deps = []
        for d in deps.values():
            if d.dma is None and dma is None and d.eng == "pe" and eng == "pe":
                continue
            o.deps.append(d)
            if d.dma is None:
                d.need_inc = True
        if dma is not None:
            self.dma_cnt[dma] = self.dma_cnt.get(dma, 0) + 16
            o.dma_val = self.dma_cnt[dma]
        self.streams[eng].append(o)
        return o

    EPOCH = 500

    def assign(self):
        neps = {}
        for e in self.ENGS:
            c = 0
            for o in self.streams[e]:
                if o.dma is None and o.need_inc:
                    o.idx = (c // self.EPOCH, c % self.EPOCH + 1)
                    c += 1
            neps[e] = (c + self.EPOCH - 1) // self.EPOCH
        return neps

    def emit(self, nc, block, esem, dsem):
        def make(name):
            ops = self.streams[name]

            def body(e):
                waited = {}
                for o in ops:
                    need = {}
                    for d in o.deps:
                        if d.dma is not None:
                            key, val, h = ("d", d.dma), d.dma_val, dsem[d.dma]
                        else:
                            key, val, h = ("e", d.eng, d.idx[0]), d.idx[1], esem[(d.eng, d.idx[0])]
                        if need.get(key, (0, None))[0] < val:
                            need[key] = (val, h)
                    for key, (val, h) in need.items():
                        if key[0] == "e":
                            later = [k for k in waited if k[0] == "e" and k[1] == key[1] and k[2] > key[2]]
                            if later:
                                continue
                        if waited.get(key, 0) < val:
                            e.wait_ge(h, val)
                            waited[key] = val
                    ins = o.fn(e)
                    if o.dma is not None:
                        ins.then_inc(dsem[o.dma], 16)
                    elif o.need_inc:
                        ins.then_inc(esem[(name, o.idx[0])], 1)
                if name == "sp":
                    for nm in sorted(self.final_dma):
                        e.wait_ge(dsem[nm], self.dma_cnt[nm])
            return body

        block.tensor(make("pe"))
        block.scalar(make("act"))
        block.vector(make("dve"))
        block.gpsimd(make("pool"))
        block.sync(make("sp"))


class Buf:
    __slots__ = ("t", "slot", "dma")

    def __init__(self, t, slot, dma=None):
        self.t = t
        self.slot = slot
        self.dma = dma


class Builder:
    def __init__(self):
        self.nc = bass.Bass("TRN2", target_bir_lowering=False)
        self.P = Prog()
        self.off = SB_BASE
        self.bank_i = 0
        self.ws_i = 0
        self.dq = []
        self.tickc = 0
        self.tick_n = 2
        self.reserved = set()
        self.diag_done = False
        self.xb = {}
        self._declare_dram()
        self._alloc()

    def sb(self, name, shape, dtype):
        sz = 4 if dtype == F32 else 2
        n = 1
        for s in shape[1:]:
            n *= s
        off = (self.off + 31) // 32 * 32
        t = self.nc.alloc_sbuf_tensor_at(name, list(shape), dtype, offset=off)
        self.off = off + n * sz
        assert self.off <= SB_LIMIT, (name, self.off)
        return t

    def _declare_dram(self):
        nc = self.nc

        def din(name, shape):
            return nc.dram_tensor(name, list(shape), F32, kind="ExternalInput").ap()
        self.x_in = din("x_in", [NHALF, XROWS, D])
        self.mem = din("mem", [NMEM, D])
        self.vecs = din("vecs", [128, NV])
        self.lnc = din("lnc", [3, D])
        self.od_w_s = din("od_w_s", [8, 128, 128])
        self.ev_w_in = din("ev_w_in", [D, 2560])
        self.ev_w_out = din("ev_w_out", [D, D])
        self.od_w_in = din("od_w_in", [D, 2048])
        self.od_w_out = din("od_w_out", [D, D])
        self.xa_w = {n: din("xa_w_" + n, [2, D, D]) for n in "qkvo"}
        self.ffn_w_gate = din("ffn_w_gate", [2, D, DFF])
        self.ffn_w_up = din("ffn_w_up", [2, D, DFF])
        self.ffn_w_down = din("ffn_w_down", [2, DFF, D])
        self.out = nc.dram_tensor("out", [TCORE, D], F32, kind="ExternalOutput").ap()

    def _alloc(self):
        nc = self.nc
        S = Slot
        self.hT = self.sb("hT", [128, KC, TH], F32)
        self.hT_s = [[S(f"hT{k}_{t}") for t in range(NTT)] for k in range(KC)]
        self.nT = self.sb("nT", [128, KC, TH], BF16)
        self.nT_s = [[S(f"nT{k}_{t}") for t in range(NTT)] for k in range(KC)]
        self.hTh = self.sb("hTh", [128, KC, 2 * HALO], F32)
        self.hTh_s = [S(f"hTh{k}") for k in range(KC)]
        self.nTh = self.sb("nTh", [128, KC, 2 * HALO], BF16)
        self.nTh_s = [S(f"nTh{k}") for k in range(KC)]
        self.ident = self.sb("ident", [128, 128], F32)
        self.ident_s = S("ident")
        self.ones = self.sb("ones", [128, 128], BF16)
        self.ones_s = S("ones")
        self.vec = self.sb("vec", [128, NV], F32)
        self.vec_s = S("vec")
        self.diagA = self.sb("diagA", [128, 4, 31, 128], BF16)
        self.diagA_s = S("diagA")
        self.diagB = self.sb("diagB", [128, 4, 3, 128], BF16)
        self.diagB_s = S("diagB")
        self.w_sT = self.sb("w_sT", [128, 8, 128], BF16)
        self.w_sT_s = S("w_sT")
        self.wbuf = [self.sb(f"wbuf{i}", [128, WSLOT], BF16) for i in range(NS)]
        self.wbuf_s = [S(f"wbuf{i}") for i in range(NS)]
        self.xs = Ring([Buf(self.sb(f"xs{i}", [128, D], F32), S(f"xs{i}"), f"xs{i}") for i in range(3)])
        self.sqr = Ring([Buf(self.sb(f"sq{i}", [128, TT], BF16), S(f"sq{i}")) for i in range(4)])
        self.rsr = Ring([Buf(self.sb(f"rs{i}", [128, TT], F32), S(f"rs{i}")) for i in range(2)])
        self.t32 = Ring([Buf(self.sb(f"t32_{i}", [128, TT], F32), S(f"t32_{i}")) for i in range(4)])
        self.memnT = self.sb("memnT", [128, 2, KC, NMEM], BF16)
        self.memnT_s = S("memnT")
        self.junk_s = S("junk")
        self.sgb = self.sb("sgb", [128, 8, 128], F32)
        self.sgb_s = S("sgb")
        self.small = self.sb("small", [128, 16], F32)
        self.small_s = S("small")
        self.epsc = self.sb("epsc", [128, 2], F32)
        self.epsc_s = S("epsc")
        u0 = self.off
        self.y = self.sb("y", [128, KC, TH], BF16)
        self.y_s = [[S(f"y{k}_{t}") for t in range(NTT)] for k in range(KC)]
        u1 = self.off
        self.off = u0
        self.mst = self.sb("mst", [128, 2, D], F32)
        self.mst_s = [S(f"mst{i}") for i in range(2)]
        self.wst = self.sb("wst", [128, 2, 512], F32)
        self.wst_s = [S(f"wst{i}") for i in range(2)]
        self.junk = self.sb("junk", [128, D], BF16)
        assert self.off <= u1
        self.off = u1
        self.a_ext = self.sb("a_ext", [128, 4, XROWS], BF16)
        self.a_ext_s = [S(f"a_ext{j}") for j in range(4)]
        self.ac = self.sb("ac", [128, 4, TT], F32)
        self.ac_s = [S(f"ac{j}") for j in range(4)]
        self.acb = self.sb("acb", [128, 4, TT], BF16)
        self.acb_s = [S(f"acb{j}") for j in range(4)]
        self.sqa = self.sb("sqa", [128, 4, TT], BF16)
        self.sqa_s = [S(f"sqa{j}") for j in range(4)]
        self.xh = self.sb("xh", [2 * HALO, D], F32)
        self.xh_s = S("xh")
        u_end = self.off
        self.off = u1
        self.v_ln = self.sb("v_ln", [128, 8, D], BF16)
        self.v_ln_s = [S(f"v_ln{c}") for c in range(8)]
        self.vg4 = self.sb("vg4", [128, 4, D], F32)
        self.vg4_s = [S(f"vg4_{c}") for c in range(4)]
        u_end = max(u_end, self.off)
        self.off = u0
        self.qT = self.sb("qT", [128, KC, TH], BF16)
        self.qT_s = [[S(f"qT{k}_{t}") for t in range(NTT)] for k in range(KC)]
        self.oT = self.sb("oT", [128, KC, TH], BF16)
        self.oT_s = [[S(f"oT{k}_{t}") for t in range(NTT)] for k in range(KC)]
        self.pT = Ring([Buf(self.sb(f"pT{i}", [128, 2, TT], BF16), S(f"pT{i}")) for i in range(2)])
        self.kT = self.sb("kT", [128, KC, NMEM], BF16)
        self.kT_s = [S(f"kT{m}") for m in range(KC)]
        self.vv = self.sb("vv", [128, 2, D], BF16)
        self.vv_s = [S(f"vv{m}") for m in range(2)]
        u_end = max(u_end, self.off)
        self.off = u0
        self.hid = self.sb("hid", [128, FC, TH], BF16)
        self.hid_s = [[S(f"hid{f}_{t}") for t in range(NTT)] for f in range(FC)]
        u_end = max(u_end, self.off)
        self.off = u0
        self.nf = [self.sb(f"nf{t}", [128, KC, TT], F32) for t in range(NTT)]
        self.nf_s = [[S(f"nf{t}_{k}") for k in range(KC)] for t in range(NTT)]
        self.os = Ring([Buf(self.sb(f"os{i}", [128, D], F32), S(f"os{i}"), f"os{i}") for i in range(4)])
        u_end = max(u_end, self.off)
        self.off = u_end
        self.union_slot = S("union")
        self.ps = [nc.alloc_psum_tensor(f"ps{i}", [128, TT], F32) for i in range(8)]
        self.ps_s = [S(f"ps{i}") for i in range(8)]

    def bank(self):
        for _ in range(16):
            b = self.bank_i % 8
            self.bank_i += 1
            if b in self.reserved:
                continue
            sl = self.ps_s[b]
            if sl.w is not None and not sl.r:
                continue
            return b
        raise RuntimeError("no free PSUM bank")

    def push(self, key, fn, lag=2):
        self.dq.append((fn, key, self.tickc + lag))

    def tick(self, n=None):
        self.tickc += 1
        n = self.tick_n if n is None else n
        while n > 0 and self.dq and self.dq[0][2] <= self.tickc:
            self.dq.pop(0)[0]()
            n -= 1

    def want(self, key):
        while any(e[1] == key for e in self.dq):
            self.dq.pop(0)[0]()

    def flush(self, keep=None):
        kept = []
        while self.dq:
            e = self.dq.pop(0)
            if keep is not None and e[1] == keep:
                kept.append(e)
            else:
                e[0]()
        self.dq = kept + self.dq

    def norm_job(self, key, src_k, src_s, n, gcol, dst_k, dst_s):
        P = self.P
        st = {"b": None, "cnt": 0, "r": None}

        def sq(k):
            if st["b"] is None:
                st["b"] = self.bank()
                self.reserved.add(st["b"])
            b = st["b"]
            ps = self.ps[b]
            first, last = st["cnt"] == 0, st["cnt"] == KC - 1
            st["cnt"] += 1
            q = self.sqr.next()
            P.add("act", lambda e: e.activation(out=q.t[:, 0:n], in_=src_k(k), func=AF.Square),
                  reads=[src_s[k]], writes=[q.slot])
            P.add("pe", lambda e: e.matmul(ps[:, 0:n], lhsT=self.ones[:, :], rhs=q.t[:, 0:n], start=first, stop=last),
                  reads=[q.slot, self.ones_s], writes=[self.ps_s[b]])

        def rstd():
            assert st["cnt"] == KC
            st["r"] = self.rstd_bc(st["b"], n, 1.0 / D, RMS_EPS)
            self.reserved.discard(st["b"])

        def scale(k):
            r = st["r"]
            P.add("dve", lambda e: e.scalar_tensor_tensor(out=dst_k(k), in0=src_k(k),
                                                          scalar=self.vec[:, gcol + k:gcol + k + 1],
                                                          in1=r.t[:, 0:n], op0=ALU.mult, op1=ALU.mult),
                  reads=[src_s[k], r.slot, self.vec_s], writes=[dst_s[k]])

        class J:
            pass
        j = J()
        j.key, j.sq, j.rstd, j.scale = key, sq, rstd, scale
        return j

    def push_finish(self, j, lag):
        self.push(j.key, j.rstd, lag)
        for k in range(0, KC, 2):
            self.push(j.key, lambda k=k: (j.scale(k), j.scale(k + 1)), lag)

    def main_job(self, tt, gcol, final=False):
        sl = slice(tt * TT, (tt + 1) * TT)
        if final:
            return self.norm_job(("n", tt), lambda k: self.hT[:, k, sl], [self.hT_s[k][tt] for k in range(KC)], TT,
                                 gcol, lambda k: self.nf[tt][:, k, :], self.nf_s[tt])
        return self.norm_job(("n", tt), lambda k: self.hT[:, k, sl], [self.hT_s[k][tt] for k in range(KC)], TT,
                             gcol, lambda k: self.nT[:, k, sl], [self.nT_s[k][tt] for k in range(KC)])

    def union_switch(self, new_slots):
        pass

    def mm_group(self, b, pairs, n, reads):
        ps = self.ps[b]
        np_ = len(pairs)

        def fn(e):
            ins = None
            for i, (l, r) in enumerate(pairs):
                ins = e.matmul(ps[:, 0:n], lhsT=l, rhs=r, start=(i == 0), stop=(i == np_ - 1))
            return ins
        o = self.P.add("pe", fn, reads=reads, writes=[self.ps_s[b]])
        self.tick()
        return o

    def ld(self, w2d, c0, cw, kc, after=()):
        i = self.ws_i % NS
        self.ws_i += 1
        assert kc * cw <= WSLOT
        view = self.wbuf[i][:, 0:kc * cw].rearrange("p (k c) -> p k c", c=cw)
        src = w2d[:, c0:c0 + cw].rearrange("(k p) c -> p k c", p=128)
        self.P.add("pool", lambda e: e.dma_start(out=view, in_=src), reads=list(after), writes=[self.wbuf_s[i]],
                   dma=f"w{i}")
        return view, self.wbuf_s[i]

    def rstd_bc(self, b, n, scale, eps, extra_sub=None):
        ps = self.ps[b]
        t = self.t32.next()
        if extra_sub is not None:
            self.P.add("dve", lambda e: e.tensor_scalar(out=t.t[:, 0:n], in0=ps[:, 0:n], scalar1=scale, scalar2=eps,
                                                        op0=ALU.mult, op1=ALU.add),
                       reads=[self.ps_s[b]], writes=[t.slot])
            self.P.add("dve", lambda e: e.tensor_tensor(out=t.t[:, 0:n], in0=t.t[:, 0:n], in1=extra_sub.t[:, 0:n],
                                                        op=ALU.subtract),
                       reads=[t.slot, extra_sub.slot], writes=[t.slot])
            self.P.add("act", lambda e: e.activation(out=t.t[:, 0:n], in_=t.t[:, 0:n], func=AF.Ln),
                       reads=[t.slot], writes=[t.slot])
        else:
            ec = 0 if eps == RMS_EPS else 1
            self.P.add("act", lambda e: e.activation(out=t.t[:, 0:n], in_=ps[:, 0:n], func=AF.Ln, scale=scale,
                                                     bias=self.epsc[:, ec:ec + 1]),
                       reads=[self.ps_s[b], self.epsc_s], writes=[t.slot])
        r = self.rsr.next()
        self.P.add("act", lambda e: e.activation(out=r.t[:, 0:n], in_=t.t[:, 0:n], func=AF.Exp, scale=-0.5),
                   reads=[t.slot], writes=[r.slot])
        return r

    def rsqrt_small(self, src, dst, tmp, scale, eps):
        ec = 0 if eps == RMS_EPS else 1
        self.P.add("act", lambda e: e.activation(out=tmp, in_=src, func=AF.Ln, scale=scale,
                                                 bias=self.epsc[:, ec:ec + 1]),
                   reads=[self.small_s, self.epsc_s], writes=[self.small_s])
        self.P.add("act", lambda e: e.activation(out=dst, in_=tmp, func=AF.Exp, scale=-0.5),
                   reads=[self.small_s], writes=[self.small_s])

    def linear(self, slab, slab_s, mloc, kc, rhs_k, rhs_s, n):
        b = self.bank()
        pairs = [(slab[:, k, mloc * 128:(mloc + 1) * 128], rhs_k(k)) for k in range(kc)]
        self.mm_group(b, pairs, n, reads=[slab_s] + list(rhs_s))
        return b

    def residual_add(self, b, m, tt):
        sl = slice(tt * TT, (tt + 1) * TT)
        ps = self.ps[b]
        self.P.add("dve", lambda e: e.tensor_tensor(out=self.hT[:, m, sl], in0=self.hT[:, m, sl], in1=ps[:, :],
                                                    op=ALU.add),
                   reads=[self.ps_s[b], self.hT_s[m][tt]], writes=[self.hT_s[m][tt]])

    def proj_residual(self, w2d, src, src_s, kc, nxt=None, final=False):
        lag = 2 if kc == KC else 1
        jobs = [self.main_job(tt, nxt, final) for tt in range(NTT)] if nxt is not None else None

        def block(slab, ss, s_, cw, tt):
            sl = slice(tt * TT, (tt + 1) * TT)
            for ml in range(cw // 128):
                m = s_ * (cw // 128) + ml
                b = self.linear(slab, ss, ml, kc, lambda k, sl=sl: src[:, k, sl],
                                [src_s[k][tt] for k in range(kc)], TT)
                self.residual_add(b, m, tt)
                if jobs is not None:
                    j = jobs[tt]
                    self.push(j.key, lambda j=j, m=m: j.sq(m), lag)
                    if m == KC - 1 and not final:
                        self.push_finish(j, lag)
        if kc == KC:
            sl0 = self.ld(w2d, 0, 512, kc)
            sl1 = self.ld(w2d, 512, 512, kc)
            block(sl0[0], sl0[1], 0, 512, 0)
            block(sl1[0], sl1[1], 1, 512, 0)
            block(sl0[0], sl0[1], 0, 512, 1)
            block(sl1[0], sl1[1], 1, 512, 1)
        else:
            for s_ in range(D // 128):
                slab, ss = self.ld(w2d, s_ * 128, 128, kc)
                for tt in range(NTT):
                    block(slab, ss, s_, 128, tt)
                    if jobs is not None and not final and s_ == KC - 1 and tt == 0:
                        self.want(jobs[0].key)
        return jobs

    def setup(self):
        P = self.P
        nc = self.nc
        P.add("sp", lambda e: e.dma_start(out=self.vec[:, :], in_=self.vecs), writes=[self.vec_s], dma="vec")
        P.add("pool", lambda e: e.memset(self.ident[:, :], 0.0), writes=[self.ident_s])
        P.add("pool", lambda e: e.affine_select(out=self.ident[:, :], in_=self.ident[:, :], pattern=[[-1, 128]],
                                                compare_op=ALU.not_equal, fill=1.0, base=0, channel_multiplier=1),
              reads=[self.ident_s], writes=[self.ident_s])
        P.add("pool", lambda e: e.memset(self.ones[:, :], 1.0), writes=[self.ones_s])
        P.add("pool", lambda e: e.memset(self.epsc[:, 0:1], RMS_EPS), writes=[self.epsc_s])
        P.add("pool", lambda e: e.memset(self.epsc[:, 1:2], LN_EPS), reads=[self.epsc_s], writes=[self.epsc_s])

    def setup_dma(self):
        P = self.P
        for mc in range(2):
            P.add("sp", lambda e, mc=mc: e.dma_start(out=self.mst[:, mc, :], in_=self.mem[mc * 128:(mc + 1) * 128, :]),
                  writes=[self.mst_s[mc]], dma=f"mst{mc}")
        for half in range(2):
            P.add("sp", lambda e, half=half: e.dma_start(
                out=self.wst[:, half, :].rearrange("p (g q) -> p g q", q=128),
                in_=self.od_w_s[half * 4:(half + 1) * 4].rearrange("g p q -> p g q")),
                writes=[self.wst_s[half]], dma=f"wst{half}")
        P.add("sp", lambda e: e.dma_start(out=self.sgb[:, :, :].rearrange("p g q -> p (g q)"),
                                          in_=self.lnc[2, :].partition_broadcast(128)),
              writes=[self.sgb_s], dma="sgb")

    def setup_ws(self, half):
        P = self.P
        b = self.bank()
        ps = self.ps[b]

        def fn(e):
            ins = None
            for i in range(4):
                ins = e.transpose(ps[:, i * 128:(i + 1) * 128], self.wst[:, half, i * 128:(i + 1) * 128],
                                  self.ident[:, :])
            return ins
        P.add("pe", fn, reads=[self.wst_s[half], self.ident_s], writes=[self.ps_s[b]])
        P.add("act", lambda e: e.activation(
            out=self.w_sT[:, half * 4:(half + 1) * 4, :], in_=ps[:, :].rearrange("p (g q) -> p g q", q=128),
            func=AF.Copy), reads=[self.ps_s[b]], writes=[self.w_sT_s])
        b2 = self.bank()
        ps2 = self.ps[b2]
        P.add("pe", lambda e: e.matmul(ps2[:, :], lhsT=self.ones[:, :], rhs=self.w_sT[:, half * 4:(half + 1) * 4, :],
                                       start=True, stop=True),
              reads=[self.w_sT_s, self.ones_s], writes=[self.ps_s[b2]])
        for gl in range(4):
            g = half * 4 + gl
            P.add("dve", lambda e, g=g, gl=gl: e.scalar_tensor_tensor(
                out=self.sgb[:, g, :], in0=ps2[:, gl * 128:(gl + 1) * 128],
                scalar=self.vec[:, V_LCB + g:V_LCB + g + 1], in1=self.sgb[:, g, :], op0=ALU.mult, op1=ALU.add),
                reads=[self.ps_s[b2], self.vec_s, self.sgb_s], writes=[self.sgb_s])

    def setup_diag(self, js, with_b, after=()):
        P = self.P
        for j in js:
            for k0 in range(0, 31, 4):
                nk = min(4, 31 - k0)
                c0 = V_CAW + j * 31 + k0
                P.add("pool", lambda e, j=j, k0=k0, nk=nk, c0=c0: e.tensor_tensor(
                    out=self.diagA[:, j, k0:k0 + nk, :],
                    in0=self.ident[:, :].unsqueeze(1).to_broadcast([128, nk, 128]),
                    in1=self.vec[:, c0:c0 + nk].unsqueeze(2).to_broadcast([128, nk, 128]),
                    op=ALU.mult), reads=[self.ident_s, self.vec_s] + list(after), writes=[self.diagA_s])
        if with_b:
            for j in range(4):
                P.add("pool", lambda e, j=j: e.tensor_tensor(
                    out=self.diagB[:, j, :, :],
                    in0=self.ident[:, :].unsqueeze(1).to_broadcast([128, 3, 128]),
                    in1=self.vec[:, V_CBW + j * 3:V_CBW + (j + 1) * 3].unsqueeze(2).to_broadcast([128, 3, 128]),
                    op=ALU.mult), reads=[self.ident_s, self.vec_s], writes=[self.diagB_s])

    def setup_mem(self, mc):
        P = self.P
        sm = self.small
        x = self.mst[:, mc, :]
        xs_ = self.mst_s[mc]
        c0 = 4 + mc * 4
        P.add("act", lambda e: e.activation(out=self.junk[:, :], in_=x, func=AF.Square, accum_out=sm[:, c0:c0 + 1]),
              reads=[xs_], writes=[self.junk_s, self.small_s])
        self.rsqrt_small(sm[:, c0:c0 + 1], sm[:, c0 + 2:c0 + 3], sm[:, c0 + 1:c0 + 2], 1.0 / D, RMS_EPS)
        P.add("dve", lambda e: e.tensor_scalar(out=x, in0=x, scalar1=sm[:, c0 + 2:c0 + 3], scalar2=None, op0=ALU.mult),
              reads=[xs_, self.small_s], writes=[xs_])
        for hf in range(2):
            b = self.bank()
            ps = self.ps[b]

            def fn(e, ps=ps, hf=hf):
                ins = None
                for i in range(4):
                    dc = hf * 4 + i
                    ins = e.transpose(ps[:, i * 128:(i + 1) * 128], self.mst[:, mc, dc * 128:(dc + 1) * 128],
                                      self.ident[:, :])
                return ins
            P.add("pe", fn, reads=[xs_, self.ident_s], writes=[self.ps_s[b]])
            for l in range(2):
                gc = V_GMEM + l * 8 + hf * 4
                P.add("dve", lambda e, ps=ps, l=l, hf=hf, gc=gc: e.tensor_tensor(
                    out=self.memnT[:, l, hf * 4:(hf + 1) * 4, mc * 128:(mc + 1) * 128],
                    in0=ps[:, :].rearrange("p (a b) -> p a b", b=128),
                    in1=self.vec[:, gc:gc + 4].unsqueeze(2).to_broadcast([128, 4, 128]), op=ALU.mult),
                    reads=[self.ps_s[b], self.vec_s], writes=[self.memnT_s])

    def x_dma(self, h, tb):
        x = self.xs.next()
        self.xb[(h, tb)] = x
        self.P.add("sp", lambda e: e.dma_start(out=x.t[:, :], in_=self.x_in[h, tb * 128:(tb + 1) * 128, :]),
                   writes=[x.slot], dma=x.dma)

    def load_prefetch(self, h):
        for tb in range(3):
            self.x_dma(h, tb)

    def load_x(self, h, st=None, fence=None, norm1=None):
        P = self.P

        def tile(tb):
            x = self.xb.pop((h, tb))
            tt = tb // 4
            for hf in range(2):
                b = self.bank()
                ps = self.ps[b]

                def fn(e, ps=ps, hf=hf):
                    ins = None
                    for i in range(4):
                        dc = hf * 4 + i
                        ins = e.transpose(ps[:, i * 128:(i + 1) * 128], x.t[:, dc * 128:(dc + 1) * 128],
                                          self.ident[:, :])
                    return ins
                P.add("pe", fn, reads=[x.slot, self.ident_s], writes=[self.ps_s[b]])
                dst = self.hT[:, hf * 4:(hf + 1) * 4, tb * 128:(tb + 1) * 128]
                src = ps[:, :].rearrange("p (a b) -> p a b", b=128)
                wr = [self.hT_s[hf * 4 + i][tt] for i in range(4)]
                if (tb + hf) % 2 == 0 and not (st is not None and tb >= 4):
                    P.add("act", lambda e, dst=dst, src=src: e.activation(out=dst, in_=src, func=AF.Copy),
                          reads=[self.ps_s[b]], writes=wr)
                else:
                    P.add("dve", lambda e, dst=dst, src=src: e.tensor_copy(out=dst, in_=src),
                          reads=[self.ps_s[b]], writes=wr)
            if tb + 3 < TH // 128:
                self.x_dma(h, tb + 3)

        def halo_dma():
            P.add("sp", lambda e: e.dma_start(out=self.xh[:, :], in_=self.x_in[h, TH:XROWS, :]),
                  writes=[self.xh_s], dma="xh")

        def halo():
            b = self.bank()
            ps = self.ps[b]

            def fn(e):
                ins = None
                for dc in range(KC):
                    ins = e.transpose(ps[:, dc * 32:(dc + 1) * 32], self.xh[:, dc * 128:(dc + 1) * 128],
                                      self.ident[0:32, 0:32])
                return ins
            P.add("pe", fn, reads=[self.xh_s, self.ident_s], writes=[self.ps_s[b]])
            P.add("dve", lambda e: e.tensor_copy(out=self.hTh[:, :, :],
                                                 in_=ps[:, 0:256].rearrange("p (a b) -> p a b", b=32)),
                  reads=[self.ps_s[b]], writes=self.hTh_s)

        if st is None:
            self.load_prefetch(h)
            halo_dma()
        for tb in range(4):
            tile(tb)
            self.tick()
        j0 = self.main_job(0, V_GMIX)
        j1 = self.main_job(1, V_GMIX)
        jh = self.norm_job("halo", lambda k: self.hTh[:, k, :], self.hTh_s, 2 * HALO, V_GMIX,
                           lambda k: self.nTh[:, k, :], self.nTh_s)
        k0, k1 = j0.key, j1.key
        if st is not None:
            for k in range(KC):
                self.push(k0, lambda k=k: j0.sq(k), 0)
            self.push(k0, norm1, 0)
            self.push(k0, j0.rstd, 0)
            self.push(k0, lambda: (j0.scale(0), j0.scale(1), j0.scale(2), j0.scale(3)), 0)
            self.push(k0, lambda: (j0.scale(4), j0.scale(5), j0.scale(6), j0.scale(7)), 0)
            for f in st:
                self.push(k0, f, 0)
            self.push(k0, fence, 0)
            self.push(k0, halo_dma, 0)
            for tb in range(4, 8):
                self.push(k1, lambda tb=tb: tile(tb), 0)
        else:
            for k in range(4):
                self.push(k0, lambda k=k: j0.sq(k), 0)
            self.push(k1, lambda: tile(4), 0)
            for k in range(4, 8):
                self.push(k0, lambda k=k: j0.sq(k), 0)
            self.push(k1, lambda: tile(5), 0)
            self.push(k0, j0.rstd, 0)
            self.push(k0, lambda: (j0.scale(0), j0.scale(1), j0.scale(2), j0.scale(3)), 0)
            self.push(k1, lambda: tile(6), 0)
            self.push(k0, lambda: (j0.scale(4), j0.scale(5), j0.scale(6), j0.scale(7)), 0)
            self.push(k1, lambda: tile(7), 0)
        for k in range(KC):
            self.push(k1, lambda k=k: j1.sq(k), 1)
        self.push_finish(j1, 1)
        self.push("halo", halo, 0)
        for k in range(KC):
            self.push("halo", lambda k=k: jh.sq(k), 1)
        self.push_finish(jh, 1)

    def store_norm(self, j, drain=True):
        if drain:
            self.want(j.key)
        j.rstd()
        for k in range(KC):
            j.scale(k)

    def store_pieces(self, h, act_tiles=()):
        P = self.P

        def piece(tt, tb):
            x = self.os.next()
            for hf in range(2):
                b = self.bank()
                ps = self.ps[b]

                def fn(e, ps=ps, hf=hf):
                    ins = None
                    for i in range(4):
                        dc = hf * 4 + i
                        ins = e.transpose(ps[:, i * 128:(i + 1) * 128], self.nf[tt][:, dc, tb * 128:(tb + 1) * 128],
                                          self.ident[:, :])
                    return ins
                rd = [self.nf_s[tt][hf * 4 + i] for i in range(4)]
                P.add("pe", fn, reads=rd + [self.ident_s], writes=[self.ps_s[b]])
                dst = x.t[:, hf * 512:(hf + 1) * 512]
                if hf == 0 or tt in act_tiles:
                    P.add("act", lambda e, dst=dst, ps=ps: e.activation(out=dst, in_=ps[:, :], func=AF.Copy),
                          reads=[self.ps_s[b]], writes=[x.slot])
                else:
                    P.add("dve", lambda e, dst=dst, ps=ps: e.tensor_copy(out=dst, in_=ps[:, :]),
                          reads=[self.ps_s[b]], writes=[x.slot])
            r0 = h * TH + tt * TT + tb * 128
            P.add("sp", lambda e: e.dma_start(out=self.out[r0:r0 + 128, :], in_=x.t[:, :]),
                  reads=[x.slot], dma=x.dma)
        P.final_dma.update(b.dma for b in self.os.items)
        return [lambda tt=tt, tb=tb: piece(tt, tb) for tt in range(NTT) for tb in range(4)]

    def l0_mixer(self):
        P = self.P
        W = self.ev_w_in
        tiles = [(tt, TT) for tt in range(NTT)] + [(-1, 2 * HALO)]

        def rhs_of(tt):
            if tt >= 0:
                self.want(("n", tt))
                sl = slice(tt * TT, (tt + 1) * TT)
                return (lambda k: self.nT[:, k, sl]), [self.nT_s[k][tt] for k in range(KC)]
            self.want("halo")
            return (lambda k: self.nTh[:, k, :]), self.nTh_s

        sva = self.ld(W, 0, 256, KC)
        sga = self.ld(W, 512, 256, KC)
        if self.diag_done:
            svb = self.ld(W, 256, 256, KC)
            sgb_ = self.ld(W, 768, 256, KC)
        else:
            self.diag_done = True
            self.setup_diag([0, 1], False)
            svb = self.ld(W, 256, 256, KC, after=[sva[1]])
            sgb_ = self.ld(W, 768, 256, KC, after=[sga[1]])
            self.want(("n", 0))
            self.setup_diag([2, 3], True, after=[self.nT_s[KC - 1][0]])
        svs, sgs = (sva, svb), (sga, sgb_)
        for tt, n in tiles:
            self.tick_n = 3 if tt == 0 else 2
            for j in range(4):
                rk, rs = rhs_of(tt)
                bv = self.linear(svs[j // 2][0], svs[j // 2][1], j % 2, KC, rk, rs, n)
                bg = self.linear(sgs[j // 2][0], sgs[j // 2][1], j % 2, KC, rk, rs, n)
                t = self.t32.next()
                P.add("act", lambda e, t=t, bg=bg, n=n: e.activation(out=t.t[:, 0:n], in_=self.ps[bg][:, 0:n],
                                                                     func=AF.Sigmoid),
                      reads=[self.ps_s[bg]], writes=[t.slot])
                if tt >= 0:
                    c0 = HALO + tt * TT
                    P.add("dve", lambda e, t=t, bv=bv, j=j, c0=c0: e.tensor_tensor(
                        out=self.a_ext[:, j, c0:c0 + TT], in0=self.ps[bv][:, :], in1=t.t[:, :], op=ALU.mult),
                        reads=[self.ps_s[bv], t.slot], writes=[self.a_ext_s[j]])
                else:
                    for (o0, i0) in ((0, 0), (HALO + TH, HALO)):
                        P.add("dve", lambda e, t=t, bv=bv, j=j, o0=o0, i0=i0: e.tensor_tensor(
                            out=self.a_ext[:, j, o0:o0 + HALO], in0=self.ps[bv][:, i0:i0 + HALO],
                            in1=t.t[:, i0:i0 + HALO], op=ALU.mult),
                            reads=[self.ps_s[bv], t.slot], writes=[self.a_ext_s[j]])
        self.flush(keep="setup")
        for tt in range(NTT):
            sl = slice(tt * TT, (tt + 1) * TT)
            cb = []
            self.tick_n = 0
            for j in range(4):
                b = self.bank()
                pairs = [(self.diagA[:, j, k, :], self.a_ext[:, j, tt * TT + k + 1:tt * TT + k + 1 + TT])
                         for k in range(31)]
                self.mm_group(b, pairs, TT, reads=[self.a_ext_s[j], self.diagA_s])
                cb.append(b)
            for j in range(4):
                b = cb[j]
                P.add("dve", lambda e, b=b, j=j: e.tensor_scalar(
                    out=self.ac[:, j, :], in0=self.ps[b][:, :], scalar1=self.vec[:, V_CAB + j:V_CAB + j + 1],
                    scalar2=None, op0=ALU.add), reads=[self.ps_s[b], self.vec_s], writes=[self.ac_s[j]])
                P.add("act", lambda e, j=j: e.activation(out=self.acb[:, j, :], in_=self.ac[:, j, :], func=AF.Copy),
                      reads=[self.ac_s[j]], writes=[self.acb_s[j]])
                P.add("act", lambda e, j=j: e.activation(out=self.sqa[:, j, :], in_=self.ac[:, j, :], func=AF.Square),
                      reads=[self.ac_s[j]], writes=[self.sqa_s[j]])
            b1 = self.bank()
            self.mm_group(b1, [(self.ones[:, :], self.acb[:, j, :]) for j in range(4)], TT,
                          reads=self.acb_s + [self.ones_s])
            b2 = self.bank()
            self.mm_group(b2, [(self.ones[:, :], self.sqa[:, j, :]) for j in range(4)], TT,
                          reads=self.sqa_s + [self.ones_s])
            self.tick_n = 2
            mean = self.t32.next()
            P.add("dve", lambda e, mean=mean, b1=b1: e.tensor_scalar(out=mean.t[:, :], in0=self.ps[b1][:, :],
                                                                     scalar1=1.0 / 512, scalar2=None, op0=ALU.mult),
                  reads=[self.ps_s[b1]], writes=[mean.slot])
            msq = self.t32.next()
            P.add("dve", lambda e, mean=mean, msq=msq: e.tensor_tensor(out=msq.t[:, :], in0=mean.t[:, :],
                                                                      in1=mean.t[:, :], op=ALU.mult),
                  reads=[mean.slot], writes=[msq.slot])
            rstd = self.rstd_bc(b2, TT, 1.0 / 512, LN_EPS, extra_sub=msq)
            if tt == 0:
                self.want("setup")
                self.phase_fence(self.mst_s + self.wst_s + [self.junk_s], self.flat(self.y_s))
            for j in range(4):
                P.add("dve", lambda e, j=j, mean=mean: e.tensor_tensor(out=self.ac[:, j, :], in0=self.ac[:, j, :],
                                                                      in1=mean.t[:, :], op=ALU.subtract),
                      reads=[self.ac_s[j], mean.slot], writes=[self.ac_s[j]])
                P.add("dve", lambda e, j=j, rstd=rstd: e.tensor_tensor(out=self.ac[:, j, :], in0=self.ac[:, j, :],
                                                                      in1=rstd.t[:, :], op=ALU.mult),
                      reads=[self.ac_s[j], rstd.slot], writes=[self.ac_s[j]])
                P.add("act", lambda e, j=j, sl=sl: e.activation(
                    out=self.y[:, j, sl], in_=self.ac[:, j, :], func=AF.Silu,
                    scale=self.vec[:, V_LAG + j:V_LAG + j + 1], bias=self.vec[:, V_LAB + j:V_LAB + j + 1]),
                    reads=[self.ac_s[j], self.vec_s], writes=[self.y_s[j][tt]])
        sh, sh_s = self.ld(W, 1024, 512, KC)
        sc, sc_s = self.ld(W, 2048, 512, KC)
        sb_, sb_s = self.ld(W, 1536, 512, KC)
        for tt, n in tiles:
            for j in range(4):
                rk, rs = rhs_of(tt)
                bh = self.linear(sh, sh_s, j, KC, rk, rs, n)
                bc = self.linear(sc, sc_s, j, KC, rk, rs, n)
                t = self.t32.next()
                P.add("act", lambda e, t=t, bh=bh, n=n: e.activation(out=t.t[:, 0:n], in_=self.ps[bh][:, 0:n],
                                                                     func=AF.Copy),
                      reads=[self.ps_s[bh]], writes=[t.slot])
                if tt >= 0:
                    c0 = 1 + tt * TT
                    P.add("dve", lambda e, t=t, bc=bc, j=j, c0=c0: e.tensor_tensor(
                        out=self.a_ext[:, j, c0:c0 + TT], in0=self.ps[bc][:, :], in1=t.t[:, :], op=ALU.mult),
                        reads=[self.ps_s[bc], t.slot], writes=[self.a_ext_s[j]])
                else:
                    for (o0, i0) in ((0, HALO - 1), (1 + TH, HALO)):
                        P.add("dve", lambda e, t=t, bc=bc, j=j, o0=o0, i0=i0: e.tensor_tensor(
                            out=self.a_ext[:, j, o0:o0 + 1], in0=self.ps[bc][:, i0:i0 + 1],
                            in1=t.t[:, i0:i0 + 1], op=ALU.mult),
                            reads=[self.ps_s[bc], t.slot], writes=[self.a_ext_s[j]])
        for tt in range(NTT):
            sl = slice(tt * TT, (tt + 1) * TT)
            for j in range(4):
                bcv = self.bank()
                pairs = [(self.diagB[:, j, k, :], self.a_ext[:, j, tt * TT + k:tt * TT + k + TT]) for k in range(3)]
                self.mm_group(bcv, pairs, TT, reads=[self.a_ext_s[j], self.diagB_s])
                rk, rs = rhs_of(tt)
                bgb = self.linear(sb_, sb_s, j, KC, rk, rs, TT)
                t = self.t32.next()
                P.add("act", lambda e, t=t, bcv=bcv, j=j: e.activation(
                    out=t.t[:, :], in_=self.ps[bcv][:, :], func=AF.Identity,
                    bias=self.vec[:, V_CBB + j:V_CBB + j + 1]),
                    reads=[self.ps_s[bcv], self.vec_s], writes=[t.slot])
                P.add("dve", lambda e, t=t, bgb=bgb, j=j, sl=sl: e.tensor_tensor(
                    out=self.y[:, 4 + j, sl], in0=self.ps[bgb][:, :], in1=t.t[:, :], op=ALU.mult),
                    reads=[self.ps_s[bgb], t.slot], writes=[self.y_s[4 + j][tt]])
        self.proj_residual(self.ev_w_out, self.y, self.y_s, KC, nxt=V_GXA)

    def l1_mixer(self):
        P = self.P
        W = self.od_w_in
        sv = [self.ld(W, 1024 + s * 512, 512, KC) for s in range(2)]
        su = [self.ld(W, s * 512, 512, KC) for s in range(2)]
        SD = self.nc.vector.BN_STATS_DIM
        sm = self.small
        mv = sm[:, 0:8].rearrange("p (c two) -> p c two", two=2)
        for tt in range(NTT):
            self.want(("n", tt))
            for cc in range(4):
                c = tt * 4 + cc
                for fh in range(2):
                    b = self.bank()
                    slab, ss = sv[fh]
                    pairs = [(self.nT[:, k, c * 128:(c + 1) * 128], slab[:, k, :]) for k in range(KC)]
                    self.mm_group(b, pairs, TT, reads=[ss] + [self.nT_s[k][tt] for k in range(KC)])
                    P.add("act", lambda e, b=b, fh=fh, cc=cc: e.activation(
                        out=self.vg4[:, cc, fh * 512:(fh + 1) * 512], in_=self.ps[b][:, :], func=AF.Gelu_apprx_tanh),
                        reads=[self.ps_s[b]], writes=[self.vg4_s[cc]])
                st = self.t32.next()
                for fh in range(2):
                    P.add("dve", lambda e, st=st, fh=fh, cc=cc: e.bn_stats(
                        out=st.t[:, fh * SD:(fh + 1) * SD], in_=self.vg4[:, cc, fh * 512:(fh + 1) * 512]),
                        reads=[self.vg4_s[cc]], writes=[st.slot])
                P.add("dve", lambda e, st=st, cc=cc: e.bn_aggr(
                    out=sm[:, 2 * cc:2 * cc + 2], in_=st.t[:, 0:2 * SD].rearrange("p (a b) -> p a b", b=SD)),
                    reads=[st.slot], writes=[self.small_s])
            P.add("act", lambda e: e.activation(out=sm[:, 8:12], in_=mv[:, :, 1], func=AF.Ln,
                                                bias=self.epsc[:, 1:2]),
                  reads=[self.small_s, self.epsc_s], writes=[self.small_s])
            P.add("act", lambda e: e.activation(out=sm[:, 12:16], in_=sm[:, 8:12], func=AF.Exp, scale=-0.5),
                  reads=[self.small_s], writes=[self.small_s])
            for cc in range(4):
                c = tt * 4 + cc
                P.add("dve", lambda e, cc=cc, c=c: e.tensor_scalar(
                    out=self.v_ln[:, c, :], in0=self.vg4[:, cc, :], scalar1=sm[:, 2 * cc:2 * cc + 1],
                    scalar2=sm[:, 12 + cc:13 + cc], op0=ALU.subtract, op1=ALU.mult),
                    reads=[self.vg4_s[cc], self.small_s], writes=[self.v_ln_s[c]])
        for tt in range(NTT):
            sl = slice(tt * TT, (tt + 1) * TT)
            for g in range(8):
                slab, ss = su[g // 4]
                bs = self.bank()
                ps = self.ps[bs]

                def fn(e, ps=ps, tt=tt, g=g):
                    ins = None
                    for cc in range(4):
                        c = tt * 4 + cc
                        ins = e.matmul(ps[:, cc * 128:(cc + 1) * 128], lhsT=self.v_ln[:, c, g * 128:(g + 1) * 128],
                                       rhs=self.w_sT[:, g, :], start=True, stop=True)
                    return ins
                P.add("pe", fn, reads=[self.v_ln_s[tt * 4 + cc] for cc in range(4)] + [self.w_sT_s],
                      writes=[self.ps_s[bs]])
                bu = self.linear(slab, ss, g % 4, KC, lambda k, sl=sl: self.nT[:, k, sl],
                                 [self.nT_s[k][tt] for k in range(KC)], TT)
                ug = self.t32.next()
                P.add("act", lambda e, ug=ug, bu=bu: e.activation(out=ug.t[:, :], in_=self.ps[bu][:, :],
                                                                  func=AF.Gelu_apprx_tanh),
                      reads=[self.ps_s[bu]], writes=[ug.slot])
                t1 = self.t32.next()
                P.add("dve", lambda e, t1=t1, ps=ps, g=g: e.scalar_tensor_tensor(
                    out=t1.t[:, :].rearrange("p (a b) -> p a b", b=128),
                    in0=ps[:, :].rearrange("p (a b) -> p a b", b=128),
                    scalar=self.vec[:, V_LCG + g:V_LCG + g + 1],
                    in1=self.sgb[:, g, :].unsqueeze(1).to_broadcast([128, 4, 128]),
                    op0=ALU.mult, op1=ALU.add), reads=[self.ps_s[bs], self.sgb_s, self.vec_s], writes=[t1.slot])
                P.add("dve", lambda e, t1=t1, ug=ug, g=g, sl=sl: e.tensor_tensor(
                    out=self.y[:, g, sl], in0=t1.t[:, :], in1=ug.t[:, :], op=ALU.mult),
                    reads=[t1.slot, ug.slot], writes=[self.y_s[g][tt]])
        self.proj_residual(self.od_w_out, self.y, self.y_s, KC, nxt=V_GXA + 8)

    def xattn(self, l):
        P = self.P
        sm = self.small
        wk, wv = self.xa_w["k"][l], self.xa_w["v"][l]
        for s in range(2):
            slab, ss = self.ld(wk, s * 512, 512, KC)
            for ml in range(4):
                m = s * 4 + ml
                b = self.linear(slab, ss, ml, KC, lambda k: self.memnT[:, l, k, :], [self.memnT_s], NMEM)
                P.add("act", lambda e, b=b, m=m: e.activation(out=self.kT[:, m, :], in_=self.ps[b][:, 0:NMEM],
                                                              func=AF.Copy),
                      reads=[self.ps_s[b]], writes=[self.kT_s[m]])
        for s in range(2):
            slab, ss = self.ld(wv, s * 512, 512, KC)
            for mc in range(2):
                b = self.bank()
                pairs = [(self.memnT[:, l, k, mc * 128:(mc + 1) * 128], slab[:, k, :]) for k in range(KC)]
                self.mm_group(b, pairs, TT, reads=[ss, self.memnT_s])
                P.add("act", lambda e, b=b, mc=mc, s=s: e.activation(out=self.vv[:, mc, s * 512:(s + 1) * 512],
                                                                     in_=self.ps[b][:, :], func=AF.Copy),
                      reads=[self.ps_s[b]], writes=[self.vv_s[mc]])
        wq = self.xa_w["q"][l]
        for s in range(2):
            slab, ss = self.ld(wq, s * 512, 512, KC)
            for tt in range(NTT):
                self.want(("n", tt))
                sl = slice(tt * TT, (tt + 1) * TT)
                for ml in range(4):
                    m = s * 4 + ml
                    b = self.linear(slab, ss, ml, KC, lambda k, sl=sl: self.nT[:, k, sl],
                                    [self.nT_s[k][tt] for k in range(KC)], TT)
                    P.add("act", lambda e, b=b, m=m, sl=sl: e.activation(out=self.qT[:, m, sl], in_=self.ps[b][:, :],
                                                                         func=AF.Copy),
                          reads=[self.ps_s[b]], writes=[self.qT_s[m][tt]])
        self.flush()
        its = [(hh, tt) for hh in range(4) for tt in range(NTT)]
        pTs = {}

        def stage_a(i):
            hh, tt = its[i]
            sl = slice(tt * TT, (tt + 1) * TT)
            pT = pTs[i] = self.pT.next()
            for mc in range(2):
                b = self.bank()
                pairs = [(self.kT[:, 2 * hh + hc, mc * 128:(mc + 1) * 128], self.qT[:, 2 * hh + hc, sl])
                         for hc in range(2)]
                self.mm_group(b, pairs, TT, reads=[self.kT_s[2 * hh], self.kT_s[2 * hh + 1],
                                                   self.qT_s[2 * hh][tt], self.qT_s[2 * hh + 1][tt]])
                P.add("act", lambda e, b=b, pT=pT, mc=mc: e.activation(out=pT.t[:, mc, :], in_=self.ps[b][:, :],
                                                                       func=AF.Exp, scale=1.0 / 16.0),
                      reads=[self.ps_s[b]], writes=[pT.slot])

        def stage_b(i):
            hh, tt = its[i]
            sl = slice(tt * TT, (tt + 1) * TT)
            pT = pTs.pop(i)
            bsum = self.bank()
            self.mm_group(bsum, [(self.ones[:, :], pT.t[:, mc, :]) for mc in range(2)], TT,
                          reads=[pT.slot, self.ones_s])
            lt = self.t32.next()
            P.add("act", lambda e: e.activation(out=lt.t[:, :], in_=self.ps[bsum][:, :], func=AF.Ln),
                  reads=[self.ps_s[bsum]], writes=[lt.slot])
            rs = self.rsr.next()
            P.add("act", lambda e: e.activation(out=rs.t[:, :], in_=lt.t[:, :], func=AF.Exp, scale=-1.0),
                  reads=[lt.slot], writes=[rs.slot])
            for oc in range(2):
                b = self.bank()
                f0 = (2 * hh + oc) * 128
                pairs = [(self.vv[:, mc, f0:f0 + 128], pT.t[:, mc, :]) for mc in range(2)]
                self.mm_group(b, pairs, TT, reads=[self.vv_s[0], self.vv_s[1], pT.slot])
                P.add("dve", lambda e, b=b, oc=oc: e.tensor_tensor(
                    out=self.oT[:, 2 * hh + oc, sl], in0=self.ps[b][:, :], in1=rs.t[:, :], op=ALU.mult),
                    reads=[self.ps_s[b], rs.slot], writes=[self.oT_s[2 * hh + oc][tt]])

        stage_a(0)
        for i in range(len(its)):
            if i + 1 < len(its):
                stage_a(i + 1)
            stage_b(i)
        self.proj_residual(self.xa_w["o"][l], self.oT, self.oT_s, KC, nxt=V_GFFN + l * 8)

    def ffn(self, l):
        P = self.P
        wg, wu, wd = self.ffn_w_gate[l], self.ffn_w_up[l], self.ffn_w_down[l]
        for s in range(6):
            cw = min(512, DFF - s * 512)
            sg, sg_s = self.ld(wg, s * 512, cw, KC)
            su, su_s = self.ld(wu, s * 512, cw, KC)
            for tt in range(NTT):
                self.want(("n", tt))
                sl = slice(tt * TT, (tt + 1) * TT)
                for fl in range(cw // 128):
                    f = s * 4 + fl
                    rk = (lambda k, sl=sl: self.nT[:, k, sl])
                    rs = [self.nT_s[k][tt] for k in range(KC)]
                    bg = self.linear(sg, sg_s, fl, KC, rk, rs, TT)
                    bu = self.linear(su, su_s, fl, KC, rk, rs, TT)
                    t = self.t32.next()
                    P.add("act", lambda e, t=t, bg=bg: e.activation(out=t.t[:, :], in_=self.ps[bg][:, :], func=AF.Silu),
                          reads=[self.ps_s[bg]], writes=[t.slot])
                    P.add("dve", lambda e, t=t, bu=bu, f=f, sl=sl: e.tensor_tensor(
                        out=self.hid[:, f, sl], in0=self.ps[bu][:, :], in1=t.t[:, :], op=ALU.mult),
                        reads=[self.ps_s[bu], t.slot], writes=[self.hid_s[f][tt]])
        if l == 0:
            return self.proj_residual(wd, self.hid, self.hid_s, FC, nxt=V_GMIX + 8)
        return self.proj_residual(wd, self.hid, self.hid_s, FC, nxt=V_GFIN, final=True)

    def phase_fence(self, old_slots, new_slots):
        ws = []
        rs = {}
        for s in old_slots:
            if s.w is not None:
                ws.append(s.w)
            for k, r in s.r.items():
                rs[(k, id(r))] = r
        for n in new_slots:
            for i, w in enumerate(ws):
                n.r[("fw", i)] = w
            for k, r in rs.items():
                n.r[("fr",) + k] = r

    def flat(self, *groups):
        out = []
        for g in groups:
            if isinstance(g, Slot):
                out.append(g)
            else:
                for x in g:
                    out.extend(self.flat(x))
        return out

    def build(self):
        nc = self.nc
        S_l0 = self.flat(self.a_ext_s, self.y_s, self.ac_s, self.acb_s, self.sqa_s, self.xh_s)
        S_l1 = self.flat(self.v_ln_s, self.vg4_s, self.y_s)
        S_xa = self.flat(self.qT_s, self.oT_s, [b.slot for b in self.pT.items], self.kT_s, self.vv_s)
        S_ff = self.flat(self.hid_s)
        S_fn = self.flat(self.nf_s, [b.slot for b in self.os.items])
        self.setup()
        prev = None
        st = None
        for h in range(NHALF):
            phases = [("l0", S_l0), ("xa0", S_xa), ("ff0", S_ff), ("l1", S_l1), ("xa1", S_xa), ("ff1", S_ff)]
            if st is None:
                self.load_x(h)
            else:
                self.load_x(h, st, lambda: self.phase_fence(S_fn, S_l0),
                            lambda j=jobs[1]: self.store_norm(j, drain=False))
            if h == 0:
                self.setup_dma()
                for i in range(2):
                    self.push("setup", lambda i=i: self.setup_mem(i), 6)
                for i in range(2):
                    self.push("setup", lambda i=i: self.setup_ws(i), 6)
            prev = S_l0
            jobs = None
            for name, slots in phases:
                if slots is not prev:
                    self.phase_fence(prev, slots)
                prev = slots
                if name == "l0":
                    self.l0_mixer()
                elif name == "l1":
                    self.l1_mixer()
                elif name.startswith("xa"):
                    self.xattn(int(name[2]))
                else:
                    if name == "ff1" and h + 1 < NHALF:
                        self.load_prefetch(h + 1)
                    jobs = self.ffn(int(name[2]))
            self.phase_fence(prev, S_fn)
            prev = S_fn
            self.store_norm(jobs[0])
            st = self.store_pieces(h, act_tiles=(0, 1) if h + 1 < NHALF else (0,))
        for f in st[:4]:
            f()
        self.store_norm(jobs[1])
        for f in st[4:]:
            f()
        self.flush()
        assert not self.reserved, self.reserved
        dma_names = sorted(self.P.dma_cnt.keys())
        from contextlib import ExitStack
        with ExitStack() as st:
            neps = self.P.assign()
            esem = {(e, i): st.enter_context(nc.semaphore(f"s_{e}{i}")) for e in Prog.ENGS for i in range(neps[e])}
            dsem = {n: st.enter_context(nc.semaphore("d_" + n)) for n in dma_names}
            block = st.enter_context(nc.Block())
            self.P.emit(nc, block, esem, dsem)
        return nc


_CACHE = {}


def _get_program():
    if "nc" not in _CACHE:
        _CACHE["nc"] = Builder().build()
    return _CACHE["nc"]


def _fm(v):
    v = np.asarray(v, np.float32)
    return np.ascontiguousarray(v.reshape(-1, 128).T)


def _host_inputs(inp):
    f = lambda a: np.ascontiguousarray(np.asarray(a, np.float32))
    x = f(inp["x"])
    mem = f(inp["mem"])
    vec = np.zeros((128, NV), np.float32)
    for i in range(2):
        vec[:, V_GMIX + i * 8:V_GMIX + (i + 1) * 8] = _fm(inp["g_mix"][i])
        vec[:, V_GXA + i * 8:V_GXA + (i + 1) * 8] = _fm(inp["g_xattn"][i])
        vec[:, V_GMEM + i * 8:V_GMEM + (i + 1) * 8] = _fm(inp["g_mem"][i])
        vec[:, V_GFFN + i * 8:V_GFFN + (i + 1) * 8] = _fm(inp["g_ffn"][i])
    vec[:, V_GFIN:V_GFIN + 8] = _fm(inp["g_final"])
    caw = np.asarray(inp["ev_a_conv_w"], np.float32)[0]
    cbw = np.asarray(inp["ev_b_conv_w"], np.float32)[0]
    for j in range(4):
        vec[:, V_CAW + j * 31:V_CAW + (j + 1) * 31] = caw[:, j * 128:(j + 1) * 128].T
        vec[:, V_CBW + j * 3:V_CBW + (j + 1) * 3] = cbw[:, j * 128:(j + 1) * 128].T
    vec[:, V_CAB:V_CAB + 4] = _fm(inp["ev_a_conv_b"][0])
    vec[:, V_LAG:V_LAG + 4] = _fm(inp["ev_a_ln_g"][0])
    vec[:, V_LAB:V_LAB + 4] = _fm(inp["ev_a_ln_b"][0])
    vec[:, V_CBB:V_CBB + 4] = _fm(inp["ev_b_conv_b"][0])
    vec[:, V_LCG:V_LCG + 8] = _fm(inp["od_c_ln_g"][0])
    vec[:, V_LCB:V_LCB + 8] = _fm(inp["od_c_ln_b"][0])
    lnc = np.stack([f(inp["od_c_ln_g"])[0], f(inp["od_c_ln_b"])[0], f(inp["od_b_s"])[0].reshape(-1)], 0)
    shared = {
        "vecs": vec, "lnc": np.ascontiguousarray(lnc), "od_w_s": f(inp["od_w_s"])[0],
        "ev_w_in": f(inp["ev_w_in"])[0], "ev_w_out": f(inp["ev_w_out"])[0],
        "od_w_in": f(inp["od_w_in"])[0], "od_w_out": f(inp["od_w_out"])[0],
        "xa_w_q": f(inp["xa_w_q"]), "xa_w_k": f(inp["xa_w_k"]), "xa_w_v": f(inp["xa_w_v"]), "xa_w_o": f(inp["xa_w_o"]),
        "ffn_w_gate": f(inp["ffn_w_gate"]), "ffn_w_up": f(inp["ffn_w_up"]), "ffn_w_down": f(inp["ffn_w_down"]),
    }
    in_maps = []
    for c in range(NCORES):
        b, s0 = c // 4, (c % 4) * TCORE
        xin = np.zeros((NHALF, XROWS, D), np.float32)
        for h in range(NHALF):
            t0 = s0 + h * TH
            xin[h, :TH] = x[b, t0:t0 + TH]
            lo = max(t0 - HALO, 0)
            xin[h, TH + HALO - (t0 - lo):TH + HALO] = x[b, lo:t0]
            hi = min(t0 + TH + HALO, SEQ)
            xin[h, TH + HALO:TH + HALO + (hi - t0 - TH)] = x[b, t0 + TH:hi]
        m = dict(shared)
        m["x_in"] = xin
        m["mem"] = mem[b]
        in_maps.append(m)
    return in_maps


def kernel(**inputs):
    nc = _get_program()
    in_maps = _host_inputs(inputs)
    res = run_bass_kernel_spmd(nc, in_maps, core_ids=list(range(NCORES)))
    out = np.empty((2, SEQ, D), np.float32)
    for c in range(NCORES):
        b, s0 = c // 4, (c % 4) * TCORE
        out[b, s0:s0 + TCORE] = np.asarray(res.results[c]["out"])
    return out
```

```python
import numpy as np
import concourse.bass as bass
import concourse.mybir as mybir
from concourse.bass_utils import run_bass_kernel_spmd

F32 = mybir.dt.float32
BF16 = mybir.dt.bfloat16
AF = mybir.ActivationFunctionType
ALU = mybir.AluOpType

NCORES = 8
D = 1024
KC = 8
SEQ = 8192
TCORE = 2048
TH = 1024
NHALF = 2
TT = 512
NTT = TH // TT
HALO = 16
XROWS = TH + 2 * HALO
DFF = 2816
FC = DFF // 128
NMEM = 256
NS = 4
WSLOT = 4096
RMS_EPS = 1e-6
LN_EPS = 1e-5
SB_BASE = 16512
SB_LIMIT = 229376

V_GMIX, V_GXA, V_GMEM, V_GFFN, V_GFIN = 0, 16, 32, 48, 64
V_CAW, V_CAB, V_LAG, V_LAB, V_CBW, V_CBB, V_LCG, V_LCB, NV = 72, 196, 200, 204, 208, 220, 224, 232, 240


class Slot:
    __slots__ = ("name", "w", "r")

    def __init__(self, name):
        self.name = name
        self.w = None
        self.r = {}


class Op:
    __slots__ = ("eng", "fn", "deps", "idx", "need_inc", "dma", "dma_val")


class Ring:
    def __init__(self, items):
        self.items = items
        self.i = 0

    def next(self):
        it = self.items[self.i % len(self.items)]
        self.i += 1
        return it


class Prog:
    ENGS = ("pe", "act", "dve", "pool", "sp")

    def __init__(self):
        self.streams = {e: [] for e in self.ENGS}
        self.dma_cnt = {}
        self.final_dma = set()

    def add(self, eng, fn, reads=(), writes=(), dma=None):
        o = Op()
        o.eng, o.fn, o.dma, o.need_inc, o.idx, o.dma_val = eng, fn, dma, False, None, None
        deps = {}
        for s in reads:
            if s.w is not None:
                deps[id(s.w)] = s.w
        for s in writes:
            if s.w is not None:
                deps[id(s.w)] = s.w
            for r in s.r.values():
                deps[id(r)] = r
        for s in reads:
            s.r[eng if dma is None else ("dma", id(o))] = o
        for s in writes:
            s.w = o
            s.r = {}
        deps.pop(id(o), None)
        o.deps = []
        for d in deps.values():
            if d.dma is None and dma is None and d.eng == "pe" and eng == "pe":
                continue
            o.deps.append(d)
            if d.dma is None:
                d.need_inc = True
        if dma is not None:
            self.dma_cnt[dma] = self.dma_cnt.get(dma, 0) + 16
            o.dma_val = self.dma_cnt[dma]
        self.streams[eng].append(o)
        return o

    EPOCH = 500

    def assign(self):
        neps = {}
        for e in self.ENGS:
            c = 0
            for o in self.streams[e]:
                if o.dma is None and o.need_inc:
                    o.idx = (c // self.EPOCH, c % self.EPOCH + 1)
                    c += 1
            neps[e] = (c + self.EPOCH - 1) // self.EPOCH
        return neps

    def emit(self, nc, block, esem, dsem):
        def make(name):
            ops = self.streams[name]

            def body(e):
                waited = {}
                for o in ops:
                    need = {}
                    for d in o.deps:
                        if d.dma is not None:
                            key, val, h = ("d", d.dma), d.dma_val, dsem[d.dma]
                        else:
                            key, val, h = ("e", d.eng, d.idx[0]), d.idx[1], esem[(d.eng, d.idx[0])]
                        if need.get(key, (0, None))[0] < val:
                            need[key] = (val, h)
                    for key, (val, h) in need.items():
                        if key[0] == "e":
                            later = [k for k in waited if k[0] == "e" and k[1] == key[1] and k[2] > key[2]]
                            if later:
                                continue
                        if waited.get(key, 0) < val:
                            e.wait_ge(h, val)
                            waited[key] = val
                    ins = o.fn(e)
                    if o.dma is not None:
                        ins.then_inc(dsem[o.dma], 16)
                    elif o.need_inc:
                        ins.then_inc(esem[(name, o.idx[0])], 1)
                if name == "sp":
                    for nm in sorted(self.final_dma):
                        e.wait_ge(dsem[nm], self.dma_cnt[nm])
            return body

        block.tensor(make("pe"))
        block.scalar(make("act"))
        block.vector(make("dve"))
        block.gpsimd(make("pool"))
        block.sync(make("sp"))


class Buf:
    __slots__ = ("t", "slot", "dma")

    def __init__(self, t, slot, dma=None):
        self.t = t
        self.slot = slot
        self.dma = dma


class Builder:
    def __init__(self):
        self.nc = bass.Bass("TRN2", target_bir_lowering=False)
        self.P = Prog()
        self.off = SB_BASE
        self.bank_i = 0
        self.ws_i = 0
        self.dq = []
        self.tickc = 0
        self.tick_n = 2
        self.reserved = set()
        self.diag_done = False
        self.xb = {}
        self._declare_dram()
        self._alloc()

    def sb(self, name, shape, dtype):
        sz = 4 if dtype == F32 else 2
        n = 1
        for s in shape[1:]:
            n *= s
        off = (self.off + 31) // 32 * 32
        t = self.nc.alloc_sbuf_tensor_at(name, list(shape), dtype, offset=off)
        self.off = off + n * sz
        assert self.off <= SB_LIMIT, (name, self.off)
        return t

    def _declare_dram(self):
        nc = self.nc

        def din(name, shape):
            return nc.dram_tensor(name, list(shape), F32, kind="ExternalInput").ap()
        self.x_in = din("x_in", [NHALF, XROWS, D])
        self.mem = din("mem", [NMEM, D])
        self.vecs = din("vecs", [128, NV])
        self.lnc = din("lnc", [3, D])
        self.od_w_s = din("od_w_s", [8, 128, 128])
        self.ev_w_in = din("ev_w_in", [D, 2560])
        self.ev_w_out = din("ev_w_out", [D, D])
        self.od_w_in = din("od_w_in", [D, 2048])
        self.od_w_out = din("od_w_out", [D, D])
        self.xa_w = {n: din("xa_w_" + n, [2, D, D]) for n in "qkvo"}
        self.ffn_w_gate = din("ffn_w_gate", [2, D, DFF])
        self.ffn_w_up = din("ffn_w_up", [2, D, DFF])
        self.ffn_w_down = din("ffn_w_down", [2, DFF, D])
        self.out = nc.dram_tensor("out", [TCORE, D], F32, kind="ExternalOutput").ap()

    def _alloc(self):
        nc = self.nc
        S = Slot
        self.hT = self.sb("hT", [128, KC, TH], F32)
        self.hT_s = [[S(f"hT{k}_{t}") for t in range(NTT)] for k in range(KC)]
        self.nT = self.sb("nT", [128, KC, TH], BF16)
        self.nT_s = [[S(f"nT{k}_{t}") for t in range(NTT)] for k in range(KC)]
        self.hTh = self.sb("hTh", [128, KC, 2 * HALO], F32)
        self.hTh_s = [S(f"hTh{k}") for k in range(KC)]
        self.nTh = self.sb("nTh", [128, KC, 2 * HALO], BF16)
        self.nTh_s = [S(f"nTh{k}") for k in range(KC)]
        self.ident = self.sb("ident", [128, 128], F32)
        self.ident_s = S("ident")
        self.ones = self.sb("ones", [128, 128], BF16)
        self.ones_s = S("ones")
        self.vec = self.sb("vec", [128, NV], F32)
        self.vec_s = S("vec")
        self.diagA = self.sb("diagA", [128, 4, 31, 128], BF16)
        self.diagA_s = S("diagA")
        self.diagB = self.sb("diagB", [128, 4, 3, 128], BF16)
        self.diagB_s = S("diagB")
        self.w_sT = self.sb("w_sT", [128, 8, 128], BF16)
        self.w_sT_s = S("w_sT")
        self.wbuf = [self.sb(f"wbuf{i}", [128, WSLOT], BF16) for i in range(NS)]
        self.wbuf_s = [S(f"wbuf{i}") for i in range(NS)]
        self.xs = Ring([Buf(self.sb(f"xs{i}", [128, D], F32), S(f"xs{i}"), f"xs{i}") for i in range(3)])
        self.sqr = Ring([Buf(self.sb(f"sq{i}", [128, TT], BF16), S(f"sq{i}")) for i in range(4)])
        self.rsr = Ring([Buf(self.sb(f"rs{i}", [128, TT], F32), S(f"rs{i}")) for i in range(2)])
        self.t32 = Ring([Buf(self.sb(f"t32_{i}", [128, TT], F32), S(f"t32_{i}")) for i in range(4)])
        self.memnT = self.sb("memnT", [128, 2, KC, NMEM], BF16)
        self.memnT_s = S("memnT")
        self.junk_s = S("junk")
        self.sgb = self.sb("sgb", [128, 8, 128], F32)
        self.sgb_s = S("sgb")
        self.small = self.sb("small", [128, 16], F32)
        self.small_s = S("small")
        self.epsc = self.sb("epsc", [128, 2], F32)
        self.epsc_s = S("epsc")
        u0 = self.off
        self.y = self.sb("y", [128, KC, TH], BF16)
        self.y_s = [[S(f"y{k}_{t}") for t in range(NTT)] for k in range(KC)]
        u1 = self.off
        self.off = u0
        self.mst = self.sb("mst", [128, 2, D], F32)
        self.mst_s = [S(f"mst{i}") for i in range(2)]
        self.wst = self.sb("wst", [128, 2, 512], F32)
        self.wst_s = [S(f"wst{i}") for i in range(2)]
        self.junk = self.sb("junk", [128, D], BF16)
        assert self.off <= u1
        self.off = u1
        self.a_ext = self.sb("a_ext", [128, 4, XROWS], BF16)
        self.a_ext_s = [S(f"a_ext{j}") for j in range(4)]
        self.ac = self.sb("ac", [128, 4, TT], F32)
        self.ac_s = [S(f"ac{j}") for j in range(4)]
        self.acb = self.sb("acb", [128, 4, TT], BF16)
        self.acb_s = [S(f"acb{j}") for j in range(4)]
        self.sqa = self.sb("sqa", [128, 4, TT], BF16)
        self.sqa_s = [S(f"sqa{j}") for j in range(4)]
        self.xh = self.sb("xh", [2 * HALO, D], F32)
        self.xh_s = S("xh")
        u_end = self.off
        self.off = u1
        self.v_ln = self.sb("v_ln", [128, 8, D], BF16)
        self.v_ln_s = [S(f"v_ln{c}") for c in range(8)]
        self.vg4 = self.sb("vg4", [128, 4, D], F32)
        self.vg4_s = [S(f"vg4_{c}") for c in range(4)]
        u_end = max(u_end, self.off)
        self.off = u0
        self.qT = self.sb("qT", [128, KC, TH], BF16)
        self.qT_s = [[S(f"qT{k}_{t}") for t in range(NTT)] for k in range(KC)]
        self.oT = self.sb("oT", [128, KC, TH], BF16)
        self.oT_s = [[S(f"oT{k}_{t}") for t in range(NTT)] for k in range(KC)]
        self.pT = Ring([Buf(self.sb(f"pT{i}", [128, 2, TT], BF16), S(f"pT{i}")) for i in range(2)])
        self.kT = self.sb("kT", [128, KC, NMEM], BF16)
        self.kT_s = [S(f"kT{m}") for m in range(KC)]
        self.vv = self.sb("vv", [128, 2, D], BF16)
        self.vv_s = [S(f"vv{m}") for m in range(2)]
        u_end = max(u_end, self.off)
        self.off = u0
        self.hid = self.sb("hid", [128, FC, TH], BF16)
        self.hid_s = [[S(f"hid{f}_{t}") for t in range(NTT)] for f in range(FC)]
        u_end = max(u_end, self.off)
        self.off = u0
        self.nf = [self.sb(f"nf{t}", [128, KC, TT], F32) for t in range(NTT)]
        self.nf_s = [[S(f"nf{t}_{k}") for k in range(KC)] for t in range(NTT)]
        self.os = Ring([Buf(self.sb(f"os{i}", [128, D], F32), S(f"os{i}"), f"os{i}") for i in range(4)])
        u_end = max(u_end, self.off)
        self.off = u_end
        self.union_slot = S("union")
        self.ps = [nc.alloc_psum_tensor(f"ps{i}", [128, TT], F32) for i in range(8)]
        self.ps_s = [S(f"ps{i}") for i in range(8)]

    def bank(self):
        for _ in range(16):
            b = self.bank_i % 8
            self.bank_i += 1
            if b in self.reserved:
                continue
            sl = self.ps_s[b]
            if sl.w is not None and not sl.r:
                continue
            return b
        raise RuntimeError("no free PSUM bank")

    def push(self, key, fn, lag=2):
        self.dq.append((fn, key, self.tickc + lag))

    def tick(self, n=None):
        self.tickc += 1
        n = self.tick_n if n is None else n
        while n > 0 and self.dq and self.dq[0][2] <= self.tickc:
            self.dq.pop(0)[0]()
            n -= 1

    def want(self, key):
        while any(e[1] == key for e in self.dq):
            self.dq.pop(0)[0]()

    def flush(self, keep=None):
        kept = []
        while self.dq:
            e = self.dq.pop(0)
            if keep is not None and e[1] == keep:
                kept.append(e)
            else:
                e[0]()
        self.dq = kept + self.dq

    def norm_job(self, key, src_k, src_s, n, gcol, dst_k, dst_s):
        P = self.P
        st = {"b": None, "cnt": 0, "r": None}

        def sq(k):
            if st["b"] is None:
                st["b"] = self.bank()
                self.reserved.add(st["b"])
            b = st["b"]
            ps = self.ps[b]
            first, last = st["cnt"] == 0, st["cnt"] == KC - 1
            st["cnt"] += 1
            q = self.sqr.next()
            P.add("act", lambda e: e.activation(out=q.t[:, 0:n], in_=src_k(k), func=AF.Square),
                  reads=[src_s[k]], writes=[q.slot])
            P.add("pe", lambda e: e.matmul(ps[:, 0:n], lhsT=self.ones[:, :], rhs=q.t[:, 0:n], start=first, stop=last),
                  reads=[q.slot, self.ones_s], writes=[self.ps_s[b]])

        def rstd():
            assert st["cnt"] == KC
            st["r"] = self.rstd_bc(st["b"], n, 1.0 / D, RMS_EPS)
            self.reserved.discard(st["b"])

        def scale(k):
            r = st["r"]
            P.add("dve", lambda e: e.scalar_tensor_tensor(out=dst_k(k), in0=src_k(k),
                                                          scalar=self.vec[:, gcol + k:gcol + k + 1],
                                                          in1=r.t[:, 0:n], op0=ALU.mult, op1=ALU.mult),
                  reads=[src_s[k], r.slot, self.vec_s], writes=[dst_s[k]])

        class J:
            pass
        j = J()
        j.key, j.sq, j.rstd, j.scale = key, sq, rstd, scale
        return j

    def push_finish(self, j, lag):
        self.push(j.key, j.rstd, lag)
        for k in range(0, KC, 2):
            self.push(j.key, lambda k=k: (j.scale(k), j.scale(k + 1)), lag)

    def main_job(self, tt, gcol, final=False):
        sl = slice(tt * TT, (tt + 1) * TT)
        if final:
            return self.norm_job(("n", tt), lambda k: self.hT[:, k, sl], [self.hT_s[k][tt] for k in range(KC)], TT,
                                 gcol, lambda k: self.nf[tt][:, k, :], self.nf_s[tt])
        return self.norm_job(("n", tt), lambda k: self.hT[:, k, sl], [self.hT_s[k][tt] for k in range(KC)], TT,
                             gcol, lambda k: self.nT[:, k, sl], [self.nT_s[k][tt] for k in range(KC)])

    def union_switch(self, new_slots):
        pass

    def mm_group(self, b, pairs, n, reads):
        ps = self.ps[b]
        np_ = len(pairs)

        def fn(e):
            ins = None
            for i, (l, r) in enumerate(pairs):
                ins = e.matmul(ps[:, 0:n], lhsT=l, rhs=r, start=(i == 0), stop=(i == np_ - 1))
            return ins
        o = self.P.add("pe", fn, reads=reads, writes=[self.ps_s[b]])
        self.tick()
        return o

    def ld(self, w2d, c0, cw, kc, after=()):
        i = self.ws_i % NS
        self.ws_i += 1
        assert kc * cw <= WSLOT
        view = self.wbuf[i][:, 0:kc * cw].rearrange("p (k c) -> p k c", c=cw)
        src = w2d[:, c0:c0 + cw].rearrange("(k p) c -> p k c", p=128)
        self.P.add("pool", lambda e: e.dma_start(out=view, in_=src), reads=list(after), writes=[self.wbuf_s[i]],
                   dma=f"w{i}")
        return view, self.wbuf_s[i]

    def rstd_bc(self, b, n, scale, eps, extra_sub=None):
        ps = self.ps[b]
        t = self.t32.next()
        if extra_sub is not None:
            self.P.add("dve", lambda e: e.tensor_scalar(out=t.t[:, 0:n], in0=ps[:, 0:n], scalar1=scale, scalar2=eps,
                                                        op0=ALU.mult, op1=ALU.add),
                       reads=[self.ps_s[b]], writes=[t.slot])
            self.P.add("dve", lambda e: e.tensor_tensor(out=t.t[:, 0:n], in0=t.t[:, 0:n], in1=extra_sub.t[:, 0:n],
                                                        op=ALU.subtract),
                       reads=[t.slot, extra_sub.slot], writes=[t.slot])
            self.P.add("act", lambda e: e.activation(out=t.t[:, 0:n], in_=t.t[:, 0:n], func=AF.Ln),
                       reads=[t.slot], writes=[t.slot])
        else:
            ec = 0 if eps == RMS_EPS else 1
            self.P.add("act", lambda e: e.activation(out=t.t[:, 0:n], in_=ps[:, 0:n], func=AF.Ln, scale=scale,
                                                     bias=self.epsc[:, ec:ec + 1]),
                       reads=[self.ps_s[b], self.epsc_s], writes=[t.slot])
        r = self.rsr.next()
        self.P.add("act", lambda e: e.activation(out=r.t[:, 0:n], in_=t.t[:, 0:n], func=AF.Exp, scale=-0.5),
                   reads=[t.slot], writes=[r.slot])
        return r

    def rsqrt_small(self, src, dst, tmp, scale, eps):
        ec = 0 if eps == RMS_EPS else 1
        self.P.add("act", lambda e: e.activation(out=tmp, in_=src, func=AF.Ln, scale=scale,
                                                 bias=self.epsc[:, ec:ec + 1]),
                   reads=[self.small_s, self.epsc_s], writes=[self.small_s])
        self.P.add("act", lambda e: e.activation(out=dst, in_=tmp, func=AF.Exp, scale=-0.5),
                   reads=[self.small_s], writes=[self.small_s])

    def linear(self, slab, slab_s, mloc, kc, rhs_k, rhs_s, n):
        b = self.bank()
        pairs = [(slab[:, k, mloc * 128:(mloc + 1) * 128], rhs_k(k)) for k in range(kc)]
        self.mm_group(b, pairs, n, reads=[slab_s] + list(rhs_s))
        return b

    def residual_add(self, b, m, tt):
        sl = slice(tt * TT, (tt + 1) * TT)
        ps = self.ps[b]
        self.P.add("dve", lambda e: e.tensor_tensor(out=self.hT[:, m, sl], in0=self.hT[:, m, sl], in1=ps[:, :],
                                                    op=ALU.add),
                   reads=[self.ps_s[b], self.hT_s[m][tt]], writes=[self.hT_s[m][tt]])

    def proj_residual(self, w2d, src, src_s, kc, nxt=None, final=False):
        lag = 2 if kc == KC else 1
        jobs = [self.main_job(tt, nxt, final) for tt in range(NTT)] if nxt is not None else None

        def block(slab, ss, s_, cw, tt):
            sl = slice(tt * TT, (tt + 1) * TT)
            for ml in range(cw // 128):
                m = s_ * (cw // 128) + ml
                b = self.linear(slab, ss, ml, kc, lambda k, sl=sl: src[:, k, sl],
                                [src_s[k][tt] for k in range(kc)], TT)
                self.residual_add(b, m, tt)
                if jobs is not None:
                    j = jobs[tt]
                    self.push(j.key, lambda j=j, m=m: j.sq(m), lag)
                    if m == KC - 1 and not final:
                        self.push_finish(j, lag)
        if kc == KC:
            sl0 = self.ld(w2d, 0, 512, kc)
            sl1 = self.ld(w2d, 512, 512, kc)
            block(sl0[0], sl0[1], 0, 512, 0)
            block(sl1[0], sl1[1], 1, 512, 0)
            block(sl0[0], sl0[1], 0, 512, 1)
            block(sl1[0], sl1[1], 1, 512, 1)
        else:
            for s_ in range(D // 128):
                slab, ss = self.ld(w2d, s_ * 128, 128, kc)
                for tt in range(NTT):
                    block(slab, ss, s_, 128, tt)
                    if jobs is not None and not final and s_ == KC - 1 and tt == 0:
                        self.want(jobs[0].key)
        return jobs

    def setup(self):
        P = self.P
        nc = self.nc
        P.add("sp", lambda e: e.dma_start(out=self.vec[:, :], in_=self.vecs), writes=[self.vec_s], dma="vec")
        P.add("pool", lambda e: e.memset(self.ident[:, :], 0.0), writes=[self.ident_s])
        P.add("pool", lambda e: e.affine_select(out=self.ident[:, :], in_=self.ident[:, :], pattern=[[-1, 128]],
                                                compare_op=ALU.not_equal, fill=1.0, base=0, channel_multiplier=1),
              reads=[self.ident_s], writes=[self.ident_s])
        P.add("pool", lambda e: e.memset(self.ones[:, :], 1.0), writes=[self.ones_s])
        P.add("pool", lambda e: e.memset(self.epsc[:, 0:1], RMS_EPS), writes=[self.epsc_s])
        P.add("pool", lambda e: e.memset(self.epsc[:, 1:2], LN_EPS), reads=[self.epsc_s], writes=[self.epsc_s])

    def setup_dma(self):
        P = self.P
        for mc in range(2):
            P.add("sp", lambda e, mc=mc: e.dma_start(out=self.mst[:, mc, :], in_=self.mem[mc * 128:(mc + 1) * 128, :]),
                  writes=[self.mst_s[mc]], dma=f"mst{mc}")
        for half in range(2):
            P.add("sp", lambda e, half=half: e.dma_start(
                out=self.wst[:, half, :].rearrange("p (g q) -> p g q", q=128),
                in_=self.od_w_s[half * 4:(half + 1) * 4].rearrange("g p q -> p g q")),
                writes=[self.wst_s[half]], dma=f"wst{half}")
        P.add("sp", lambda e: e.dma_start(out=self.sgb[:, :, :].rearrange("p g q -> p (g q)"),
                                          in_=self.lnc[2, :].partition_broadcast(128)),
              writes=[self.sgb_s], dma="sgb")

    def setup_ws(self, half):
        P = self.P
        b = self.bank()
        ps = self.ps[b]

        def fn(e):
            ins = None
            for i in range(4):
                ins = e.transpose(ps[:, i * 128:(i + 1) * 128], self.wst[:, half, i * 128:(i + 1) * 128],
                                  self.ident[:, :])
            return ins
        P.add("pe", fn, reads=[self.wst_s[half], self.ident_s], writes=[self.ps_s[b]])
        P.add("act", lambda e: e.activation(
            out=self.w_sT[:, half * 4:(half + 1) * 4, :], in_=ps[:, :].rearrange("p (g q) -> p g q", q=128),
            func=AF.Copy), reads=[self.ps_s[b]], writes=[self.w_sT_s])
        b2 = self.bank()
        ps2 = self.ps[b2]
        P.add("pe", lambda e: e.matmul(ps2[:, :], lhsT=self.ones[:, :], rhs=self.w_sT[:, half * 4:(half + 1) * 4, :],
                                       start=True, stop=True),
              reads=[self.w_sT_s, self.ones_s], writes=[self.ps_s[b2]])
        for gl in range(4):
            g = half * 4 + gl
            P.add("dve", lambda e, g=g, gl=gl: e.scalar_tensor_tensor(
                out=self.sgb[:, g, :], in0=ps2[:, gl * 128:(gl + 1) * 128],
                scalar=self.vec[:, V_LCB + g:V_LCB + g + 1], in1=self.sgb[:, g, :], op0=ALU.mult, op1=ALU.add),
                reads=[self.ps_s[b2], self.vec_s, self.sgb_s], writes=[self.sgb_s])

    def setup_diag(self, js, with_b, after=()):
        P = self.P
        for j in js:
            for k0 in range(0, 31, 4):
                nk = min(4, 31 - k0)
                c0 = V_CAW + j * 31 + k0
                P.add("pool", lambda e, j=j, k0=k0, nk=nk, c0=c0: e.tensor_tensor(
                    out=self.diagA[:, j, k0:k0 + nk, :],
                    in0=self.ident[:, :].unsqueeze(1).to_broadcast([128, nk, 128]),
                    in1=self.vec[:, c0:c0 + nk].unsqueeze(2).to_broadcast([128, nk, 128]),
                    op=ALU.mult), reads=[self.ident_s, self.vec_s] + list(after), writes=[self.diagA_s])
        if with_b:
            for j in range(4):
                P.add("pool", lambda e, j=j: e.tensor_tensor(
                    out=self.diagB[:, j, :, :],
                    in0=self.ident[:, :].unsqueeze(1).to_broadcast([128, 3, 128]),
                    in1=self.vec[:, V_CBW + j * 3:V_CBW + (j + 1) * 3].unsqueeze(2).to_broadcast([128, 3, 128]),
                    op=ALU.mult), reads=[self.ident_s, self.vec_s], writes=[self.diagB_s])

    def setup_mem(self, mc):
        P = self.P
        sm = self.small
        x = self.mst[:, mc, :]
        xs_ = self.mst_s[mc]
        c0 = 4 + mc * 4
        P.add("act", lambda e: e.activation(out=self.junk[:, :], in_=x, func=AF.Square, accum_out=sm[:, c0:c0 + 1]),
              reads=[xs_], writes=[self.junk_s, self.small_s])
        self.rsqrt_small(sm[:, c0:c0 + 1], sm[:, c0 + 2:c0 + 3], sm[:, c0 + 1:c0 + 2], 1.0 / D, RMS_EPS)
        P.add("dve", lambda e: e.tensor_scalar(out=x, in0=x, scalar1=sm[:, c0 + 2:c0 + 3], scalar2=None, op0=ALU.mult),
              reads=[xs_, self.small_s], writes=[xs_])
        for hf in range(2):
            b = self.bank()
            ps = self.ps[b]

            def fn(e, ps=ps, hf=hf):
                ins = None
                for i in range(4):
                    dc = hf * 4 + i
                    ins = e.transpose(ps[:, i * 128:(i + 1) * 128], self.mst[:, mc, dc * 128:(dc + 1) * 128],
                                      self.ident[:, :])
                return ins
            P.add("pe", fn, reads=[xs_, self.ident_s], writes=[self.ps_s[b]])
            for l in range(2):
                gc = V_GMEM + l * 8 + hf * 4
                P.add("dve", lambda e, ps=ps, l=l, hf=hf, gc=gc: e.tensor_tensor(
                    out=self.memnT[:, l, hf * 4:(hf + 1) * 4, mc * 128:(mc + 1) * 128],
                    in0=ps[:, :].rearrange("p (a b) -> p a b", b=128),
                    in1=self.vec[:, gc:gc + 4].unsqueeze(2).to_broadcast([128, 4, 128]), op=ALU.mult),
                    reads=[self.ps_s[b], self.vec_s], writes=[self.memnT_s])

    def x_dma(self, h, tb):
        x = self.xs.next()
        self.xb[(h, tb)] = x
        self.P.add("sp", lambda e: e.dma_start(out=x.t[:, :], in_=self.x_in[h, tb * 128:(tb + 1) * 128, :]),
                   writes=[x.slot], dma=x.dma)

    def load_prefetch(self, h):
        for tb in range(3):
            self.x_dma(h, tb)

    def load_x(self, h, st=None, fence=None, norm1=None):
        P = self.P

        def tile(tb):
            x = self.xb.pop((h, tb))
            tt = tb // 4
            for hf in range(2):
                b = self.bank()
                ps = self.ps[b]

                def fn(e, ps=ps, hf=hf):
                    ins = None
                    for i in range(4):
                        dc = hf * 4 + i
                        ins = e.transpose(ps[:, i * 128:(i + 1) * 128], x.t[:, dc * 128:(dc + 1) * 128],
                                          self.ident[:, :])
                    return ins
                P.add("pe", fn, reads=[x.slot, self.ident_s], writes=[self.ps_s[b]])
                dst = self.hT[:, hf * 4:(hf + 1) * 4, tb * 128:(tb + 1) * 128]
                src = ps[:, :].rearrange("p (a b) -> p a b", b=128)
                wr = [self.hT_s[hf * 4 + i][tt] for i in range(4)]
                if (tb + hf) % 2 == 0 and not (st is not None and tb >= 4):
                    P.add("act", lambda e, dst=dst, src=src: e.activation(out=dst, in_=src, func=AF.Copy),
                          reads=[self.ps_s[b]], writes=wr)
                else:
                    P.add("dve", lambda e, dst=dst, src=src: e.tensor_copy(out=dst, in_=src),
                          reads=[self.ps_s[b]], writes=wr)
            if tb + 3 < TH // 128:
                self.x_dma(h, tb + 3)

        def halo_dma():
            P.add("sp", lambda e: e.dma_start(out=self.xh[:, :], in_=self.x_in[h, TH:XROWS, :]),
                  writes=[self.xh_s], dma="xh")

        def halo():
            b = self.bank()
            ps = self.ps[b]

            def fn(e):
                ins = None
                for dc in range(KC):
                    ins = e.transpose(ps[:, dc * 32:(dc + 1) * 32], self.xh[:, dc * 128:(dc + 1) * 128],
                                      self.ident[0:32, 0:32])
                return ins
            P.add("pe", fn, reads=[self.xh_s, self.ident_s], writes=[self.ps_s[b]])
            P.add("dve", lambda e: e.tensor_copy(out=self.hTh[:, :, :],
                                                 in_=ps[:, 0:256].rearrange("p (a b) -> p a b", b=32)),
                  reads=[self.ps_s[b]], writes=self.hTh_s)

        if st is None:
            self.load_prefetch(h)
            halo_dma()
        for tb in range(4):
            tile(tb)
            self.tick()
        j0 = self.main_job(0, V_GMIX)
        j1 = self.main_job(1, V_GMIX)
        jh = self.norm_job("halo", lambda k: self.hTh[:, k, :], self.hTh_s, 2 * HALO, V_GMIX,
                           lambda k: self.nTh[:, k, :], self.nTh_s)
        k0, k1 = j0.key, j1.key
        if st is not None:
            for k in range(KC):
                self.push(k0, lambda k=k: j0.sq(k), 0)
            self.push(k0, norm1, 0)
            self.push(k0, j0.rstd, 0)
            self.push(k0, lambda: (j0.scale(0), j0.scale(1), j0.scale(2), j0.scale(3)), 0)
            self.push(k0, lambda: (j0.scale(4), j0.scale(5), j0.scale(6), j0.scale(7)), 0)
            for f in st:
                self.push(k0, f, 0)
            self.push(k0, fence, 0)
            self.push(k0, halo_dma, 0)
            for tb in range(4, 8):
                self.push(k1, lambda tb=tb: tile(tb), 0)
        else:
            for k in range(4):
                self.push(k0, lambda k=k: j0.sq(k), 0)
            self.push(k1, lambda: tile(4), 0)
            for k in range(4, 8):
                self.push(k0, lambda k=k: j0.sq(k), 0)
            self.push(k1, lambda: tile(5), 0)
            self.push(k0, j0.rstd, 0)
            self.push(k0, lambda: (j0.scale(0), j0.scale(1), j0.scale(2), j0.scale(3)), 0)
            self.push(k1, lambda: tile(6), 0)
            self.push(k0, lambda: (j0.scale(4), j0.scale(5), j0.scale(6), j0.scale(7)), 0)
            self.push(k1, lambda: tile(7), 0)
        for k in range(KC):
            self.push(k1, lambda k=k: j1.sq(k), 1)
        self.push_finish(j1, 1)
        self.push("halo", halo, 0)
        for k in range(KC):
            self.push("halo", lambda k=k: jh.sq(k), 1)
        self.push_finish(jh, 1)

    def store_norm(self, j, drain=True):
        if drain:
            self.want(j.key)
        j.rstd()
        for k in range(KC):
            j.scale(k)

    def store_pieces(self, h, act_tiles=()):
        P = self.P

        def piece(tt, tb):
            x = self.os.next()
            for hf in range(2):
                b = self.bank()
                ps = self.ps[b]

                def fn(e, ps=ps, hf=hf):
                    ins = None
                    for i in range(4):
                        dc = hf * 4 + i
                        ins = e.transpose(ps[:, i * 128:(i + 1) * 128], self.nf[tt][:, dc, tb * 128:(tb + 1) * 128],
                                          self.ident[:, :])
                    return ins
                rd = [self.nf_s[tt][hf * 4 + i] for i in range(4)]
                P.add("pe", fn, reads=rd + [self.ident_s], writes=[self.ps_s[b]])
                dst = x.t[:, hf * 512:(hf + 1) * 512]
                if hf == 0 or tt in act_tiles:
                    P.add("act", lambda e, dst=dst, ps=ps: e.activation(out=dst, in_=ps[:, :], func=AF.Copy),
                          reads=[self.ps_s[b]], writes=[x.slot])
                else:
                    P.add("dve", lambda e, dst=dst, ps=ps: e.tensor_copy(out=dst, in_=ps[:, :]),
                          reads=[self.ps_s[b]], writes=[x.slot])
            r0 = h * TH + tt * TT + tb * 128
            P.add("sp", lambda e: e.dma_start(out=self.out[r0:r0 + 128, :], in_=x.t[:, :]),
                  reads=[x.slot], dma=x.dma)
        P.final_dma.update(b.dma for b in self.os.items)
        return [lambda tt=tt, tb=tb: piece(tt, tb) for tt in range(NTT) for tb in range(4)]

    def l0_mixer(self):
        P = self.P
        W = self.ev_w_in
        tiles = [(tt, TT) for tt in range(NTT)] + [(-1, 2 * HALO)]

        def rhs_of(tt):
            if tt >= 0:
                self.want(("n", tt))
                sl = slice(tt * TT, (tt + 1) * TT)
                return (lambda k: self.nT[:, k, sl]), [self.nT_s[k][tt] for k in range(KC)]
            self.want("halo")
            return (lambda k: self.nTh[:, k, :]), self.nTh_s

        sva = self.ld(W, 0, 256, KC)
        sga = self.ld(W, 512, 256, KC)
        if self.diag_done:
            svb = self.ld(W, 256, 256, KC)
            sgb_ = self.ld(W, 768, 256, KC)
        else:
            self.diag_done = True
            self.setup_diag([0, 1], False)
            svb = self.ld(W, 256, 256, KC, after=[sva[1]])
            sgb_ = self.ld(W, 768, 256, KC, after=[sga[1]])
            self.want(("n", 0))
            self.setup_diag([2, 3], True, after=[self.nT_s[KC - 1][0]])
        svs, sgs = (sva, svb), (sga, sgb_)
        for tt, n in tiles:
            self.tick_n = 3 if tt == 0 else 2
            for j in range(4):
                rk, rs = rhs_of(tt)
                bv = self.linear(svs[j // 2][0], svs[j // 2][1], j % 2, KC, rk, rs, n)
                bg = self.linear(sgs[j // 2][0], sgs[j // 2][1], j % 2, KC, rk, rs, n)
                t = self.t32.next()
                P.add("act", lambda e, t=t, bg=bg, n=n: e.activation(out=t.t[:, 0:n], in_=self.ps[bg][:, 0:n],
                                                                     func=AF.Sigmoid),
                      reads=[self.ps_s[bg]], writes=[t.slot])
                if tt >= 0:
                    c0 = HALO + tt * TT
                    P.add("dve", lambda e, t=t, bv=bv, j=j, c0=c0: e.tensor_tensor(
                        out=self.a_ext[:, j, c0:c0 + TT], in0=self.ps[bv][:, :], in1=t.t[:, :], op=ALU.mult),
                        reads=[self.ps_s[bv], t.slot], writes=[self.a_ext_s[j]])
                else:
                    for (o0, i0) in ((0, 0), (HALO + TH, HALO)):
                        P.add("dve", lambda e, t=t, bv=bv, j=j, o0=o0, i0=i0: e.tensor_tensor(
                            out=self.a_ext[:, j, o0:o0 + HALO], in0=self.ps[bv][:, i0:i0 + HALO],
                            in1=t.t[:, i0:i0 + HALO], op=ALU.mult),
                            reads=[self.ps_s[bv], t.slot], writes=[self.a_ext_s[j]])
        self.flush(keep="setup")
        for tt in range(NTT):
            sl = slice(tt * TT, (tt + 1) * TT)
            cb = []
            self.tick_n = 0
            for j in range(4):
                b = self.bank()
                pairs = [(self.diagA[:, j, k, :], self.a_ext[:, j, tt * TT + k + 1:tt * TT + k + 1 + TT])
                         for k in range(31)]
                self.mm_group(b, pairs, TT, reads=[self.a_ext_s[j], self.diagA_s])
                cb.append(b)
            for j in range(4):
                b = cb[j]
                P.add("dve", lambda e, b=b, j=j: e.tensor_scalar(
                    out=self.ac[:, j, :], in0=self.ps[b][:, :], scalar1=self.vec[:, V_CAB + j:V_CAB + j + 1],
                    scalar2=None, op0=ALU.add), reads=[self.ps_s[b], self.vec_s], writes=[self.ac_s[j]])
                P.add("act", lambda e, j=j: e.activation(out=self.acb[:, j, :], in_=self.ac[:, j, :], func=AF.Copy),
                      reads=[self.ac_s[j]], writes=[self.acb_s[j]])
                P.add("act", lambda e, j=j: e.activation(out=self.sqa[:, j, :], in_=self.ac[:, j, :], func=AF.Square),
                      reads=[self.ac_s[j]], writes=[self.sqa_s[j]])
            b1 = self.bank()
            self.mm_group(b1, [(self.ones[:, :], self.acb[:, j, :]) for j in range(4)], TT,
                          reads=self.acb_s + [self.ones_s])
            b2 = self.bank()
            self.mm_group(b2, [(self.ones[:, :], self.sqa[:, j, :]) for j in range(4)], TT,
                          reads=self.sqa_s + [self.ones_s])
            self.tick_n = 2
            mean = self.t32.next()
            P.add("dve", lambda e, mean=mean, b1=b1: e.tensor_scalar(out=mean.t[:, :], in0=self.ps[b1][:, :],
                                                                     scalar1=1.0 / 512, scalar2=None, op0=ALU.mult),
                  reads=[self.ps_s[b1]], writes=[mean.slot])
            msq = self.t32.next()
            P.add("dve", lambda e, mean=mean, msq=msq: e.tensor_tensor(out=msq.t[:, :], in0=mean.t[:, :],
                                                                      in1=mean.t[:, :], op=ALU.mult),
                  reads=[mean.slot], writes=[msq.slot])
            rstd = self.rstd_bc(b2, TT, 1.0 / 512, LN_EPS, extra_sub=msq)
            if tt == 0:
                self.want("setup")
                self.phase_fence(self.mst_s + self.wst_s + [self.junk_s], self.flat(self.y_s))
            for j in range(4):
                P.add("dve", lambda e, j=j, mean=mean: e.tensor_tensor(out=self.ac[:, j, :], in0=self.ac[:, j, :],
                                                                      in1=mean.t[:, :], op=ALU.subtract),
                      reads=[self.ac_s[j], mean.slot], writes=[self.ac_s[j]])
                P.add("dve", lambda e, j=j, rstd=rstd: e.tensor_tensor(out=self.ac[:, j, :], in0=self.ac[:, j, :],
                                                                      in1=rstd.t[:, :], op=ALU.mult),
                      reads=[self.ac_s[j], rstd.slot], writes=[self.ac_s[j]])
                P.add("act", lambda e, j=j, sl=sl: e.activation(
                    out=self.y[:, j, sl], in_=self.ac[:, j, :], func=AF.Silu,
                    scale=self.vec[:, V_LAG + j:V_LAG + j + 1], bias=self.vec[:, V_LAB + j:V_LAB + j + 1]),
                    reads=[self.ac_s[j], self.vec_s], writes=[self.y_s[j][tt]])
        sh, sh_s = self.ld(W, 1024, 512, KC)
        sc, sc_s = self.ld(W, 2048, 512, KC)
        sb_, sb_s = self.ld(W, 1536, 512, KC)
        for tt, n in tiles:
            for j in range(4):
                rk, rs = rhs_of(tt)
                bh = self.linear(sh, sh_s, j, KC, rk, rs, n)
                bc = self.linear(sc, sc_s, j, KC, rk, rs, n)
                t = self.t32.next()
                P.add("act", lambda e, t=t, bh=bh, n=n: e.activation(out=t.t[:, 0:n], in_=self.ps[bh][:, 0:n],
                                                                     func=AF.Copy),
                      reads=[self.ps_s[bh]], writes=[t.slot])
                if tt >= 0:
                    c0 = 1 + tt * TT
                    P.add("dve", lambda e, t=t, bc=bc, j=j, c0=c0: e.tensor_tensor(
                        out=self.a_ext[:, j, c0:c0 + TT], in0=self.ps[bc][:, :], in1=t.t[:, :], op=ALU.mult),
                        reads=[self.ps_s[bc], t.slot], writes=[self.a_ext_s[j]])
                else:
                    for (o0, i0) in ((0, HALO - 1), (1 + TH, HALO)):
                        P.add("dve", lambda e, t=t, bc=bc, j=j, o0=o0, i0=i0: e.tensor_tensor(
                            out=self.a_ext[:, j, o0:o0 + 1], in0=self.ps[bc][:, i0:i0 + 1],
                            in1=t.t[:, i0:i0 + 1], op=ALU.mult),
                            reads=[self.ps_s[bc], t.slot], writes=[self.a_ext_s[j]])
        for tt in range(NTT):
            sl = slice(tt * TT, (tt + 1) * TT)
            for j in range(4):
                bcv = self.bank()
                pairs = [(self.diagB[:, j, k, :], self.a_ext[:, j, tt * TT + k:tt * TT + k + TT]) for k in range(3)]
                self.mm_group(bcv, pairs, TT, reads=[self.a_ext_s[j], self.diagB_s])
                rk, rs = rhs_of(tt)
                bgb = self.linear(sb_, sb_s, j, KC, rk, rs, TT)
                t = self.t32.next()
                P.add("act", lambda e, t=t, bcv=bcv, j=j: e.activation(
                    out=t.t[:, :], in_=self.ps[bcv][:, :], func=AF.Identity,
                    bias=self.vec[:, V_CBB + j:V_CBB + j + 1]),
                    reads=[self.ps_s[bcv], self.vec_s], writes=[t.slot])
                P.add("dve", lambda e, t=t, bgb=bgb, j=j, sl=sl: e.tensor_tensor(
                    out=self.y[:, 4 + j, sl], in0=self.ps[bgb][:, :], in1=t.t[:, :], op=ALU.mult),
                    reads=[self.ps_s[bgb], t.slot], writes=[self.y_s[4 + j][tt]])
        self.proj_residual(self.ev_w_out, self.y, self.y_s, KC, nxt=V_GXA)

    def l1_mixer(self):
        P = self.P
        W = self.od_w_in
        sv = [self.ld(W, 1024 + s * 512, 512, KC) for s in range(2)]
        su = [self.ld(W, s * 512, 512, KC) for s in range(2)]
        SD = self.nc.vector.BN_STATS_DIM
        sm = self.small
        mv = sm[:, 0:8].rearrange("p (c two) -> p c two", two=2)
        for tt in range(NTT):
            self.want(("n", tt))
            for cc in range(4):
                c = tt * 4 + cc
                for fh in range(2):
                    b = self.bank()
                    slab, ss = sv[fh]
                    pairs = [(self.nT[:, k, c * 128:(c + 1) * 128], slab[:, k, :]) for k in range(KC)]
                    self.mm_group(b, pairs, TT, reads=[ss] + [self.nT_s[k][tt] for k in range(KC)])
                    P.add("act", lambda e, b=b, fh=fh, cc=cc: e.activation(
                        out=self.vg4[:, cc, fh * 512:(fh + 1) * 512], in_=self.ps[b][:, :], func=AF.Gelu_apprx_tanh),
                        reads=[self.ps_s[b]], writes=[self.vg4_s[cc]])
                st = self.t32.next()
                for fh in range(2):
                    P.add("dve", lambda e, st=st, fh=fh, cc=cc: e.bn_stats(
                        out=st.t[:, fh * SD:(fh + 1) * SD], in_=self.vg4[:, cc, fh * 512:(fh + 1) * 512]),
                        reads=[self.vg4_s[cc]], writes=[st.slot])
                P.add("dve", lambda e, st=st, cc=cc: e.bn_aggr(
                    out=sm[:, 2 * cc:2 * cc + 2], in_=st.t[:, 0:2 * SD].rearrange("p (a b) -> p a b", b=SD)),
                    reads=[st.slot], writes=[self.small_s])
            P.add("act", lambda e: e.activation(out=sm[:, 8:12], in_=mv[:, :, 1], func=AF.Ln,
                                                bias=self.epsc[:, 1:2]),
                  reads=[self.small_s, self.epsc_s], writes=[self.small_s])
            P.add("act", lambda e: e.activation(out=sm[:, 12:16], in_=sm[:, 8:12], func=AF.Exp, scale=-0.5),
                  reads=[self.small_s], writes=[self.small_s])
            for cc in range(4):
                c = tt * 4 + cc
                P.add("dve", lambda e, cc=cc, c=c: e.tensor_scalar(
                    out=self.v_ln[:, c, :], in0=self.vg4[:, cc, :], scalar1=sm[:, 2 * cc:2 * cc + 1],
                    scalar2=sm[:, 12 + cc:13 + cc], op0=ALU.subtract, op1=ALU.mult),
                    reads=[self.vg4_s[cc], self.small_s], writes=[self.v_ln_s[c]])
        for tt in range(NTT):
            sl = slice(tt * TT, (tt + 1) * TT)
            for g in range(8):
                slab, ss = su[g // 4]
                bs = self.bank()
                ps = self.ps[bs]

                def fn(e, ps=ps, tt=tt, g=g):
                    ins = None
                    for cc in range(4):
                        c = tt * 4 + cc
                        ins = e.matmul(ps[:, cc * 128:(cc + 1) * 128], lhsT=self.v_ln[:, c, g * 128:(g + 1) * 128],
                                       rhs=self.w_sT[:, g, :], start=True, stop=True)
                    return ins
                P.add("pe", fn, reads=[self.v_ln_s[tt * 4 + cc] for cc in range(4)] + [self.w_sT_s],
                      writes=[self.ps_s[bs]])
                bu = self.linear(slab, ss, g % 4, KC, lambda k, sl=sl: self.nT[:, k, sl],
                                 [self.nT_s[k][tt] for k in range(KC)], TT)
                ug = self.t32.next()
                P.add("act", lambda e, ug=ug, bu=bu: e.activation(out=ug.t[:, :], in_=self.ps[bu][:, :],
                                                                  func=AF.Gelu_apprx_tanh),
                      reads=[self.ps_s[bu]], writes=[ug.slot])
                t1 = self.t32.next()
                P.add("dve", lambda e, t1=t1, ps=ps, g=g: e.scalar_tensor_tensor(
                    out=t1.t[:, :].rearrange("p (a b) -> p a b", b=128),
                    in0=ps[:, :].rearrange("p (a b) -> p a b", b=128),
                    scalar=self.vec[:, V_LCG + g:V_LCG + g + 1],
                    in1=self.sgb[:, g, :].unsqueeze(1).to_broadcast([128, 4, 128]),
                    op0=ALU.mult, op1=ALU.add), reads=[self.ps_s[bs], self.sgb_s, self.vec_s], writes=[t1.slot])
                P.add("dve", lambda e, t1=t1, ug=ug, g=g, sl=sl: e.tensor_tensor(
                    out=self.y[:, g, sl], in0=t1.t[:, :], in1=ug.t[:, :], op=ALU.mult),
                    reads=[t1.slot, ug.slot], writes=[self.y_s[g][tt]])
        self.proj_residual(self.od_w_out, self.y, self.y_s, KC, nxt=V_GXA + 8)

    def xattn(self, l):
        P = self.P
        sm = self.small
        wk, wv = self.xa_w["k"][l], self.xa_w["v"][l]
        for s in range(2):
            slab, ss = self.ld(wk, s * 512, 512, KC)
            for ml in range(4):
                m = s * 4 + ml
                b = self.linear(slab, ss, ml, KC, lambda k: self.memnT[:, l, k, :], [self.memnT_s], NMEM)
                P.add("act", lambda e, b=b, m=m: e.activation(out=self.kT[:, m, :], in_=self.ps[b][:, 0:NMEM],
                                                              func=AF.Copy),
                      reads=[self.ps_s[b]], writes=[self.kT_s[m]])
        for s in range(2):
            slab, ss = self.ld(wv, s * 512, 512, KC)
            for mc in range(2):
                b = self.bank()
                pairs = [(self.memnT[:, l, k, mc * 128:(mc + 1) * 128], slab[:, k, :]) for k in range(KC)]
                self.mm_group(b, pairs, TT, reads=[ss, self.memnT_s])
                P.add("act", lambda e, b=b, mc=mc, s=s: e.activation(out=self.vv[:, mc, s * 512:(s + 1) * 512],
                                                                     in_=self.ps[b][:, :], func=AF.Copy),
                      reads=[self.ps_s[b]], writes=[self.vv_s[mc]])
        wq = self.xa_w["q"][l]
        for s in range(2):
            slab, ss = self.ld(wq, s * 512, 512, KC)
            for tt in range(NTT):
                self.want(("n", tt))
                sl = slice(tt * TT, (tt + 1) * TT)
                for ml in range(4):
                    m = s * 4 + ml
                    b = self.linear(slab, ss, ml, KC, lambda k, sl=sl: self.nT[:, k, sl],
                                    [self.nT_s[k][tt] for k in range(KC)], TT)
                    P.add("act", lambda e, b=b, m=m, sl=sl: e.activation(out=self.qT[:, m, sl], in_=self.ps[b][:, :],
                                                                         func=AF.Copy),
                          reads=[self.ps_s[b]], writes=[self.qT_s[m][tt]])
        self.flush()
        its = [(hh, tt) for hh in range(4) for tt in range(NTT)]
        pTs = {}

        def stage_a(i):
            hh, tt = its[i]
            sl = slice(tt * TT, (tt + 1) * TT)
            pT = pTs[i] = self.pT.next()
            for mc in range(2):
                b = self.bank()
                pairs = [(self.kT[:, 2 * hh + hc, mc * 128:(mc + 1) * 128], self.qT[:, 2 * hh + hc, sl])
                         for hc in range(2)]
                self.mm_group(b, pairs, TT, reads=[self.kT_s[2 * hh], self.kT_s[2 * hh + 1],
                                                   self.qT_s[2 * hh][tt], self.qT_s[2 * hh + 1][tt]])
                P.add("act", lambda e, b=b, pT=pT, mc=mc: e.activation(out=pT.t[:, mc, :], in_=self.ps[b][:, :],
                                                                       func=AF.Exp, scale=1.0 / 16.0),
                      reads=[self.ps_s[b]], writes=[pT.slot])

        def stage_b(i):
            hh, tt = its[i]
            sl = slice(tt * TT, (tt + 1) * TT)
            pT = pTs.pop(i)
            bsum = self.bank()
            self.mm_group(bsum, [(self.ones[:, :], pT.t[:, mc, :]) for mc in range(2)], TT,
                          reads=[pT.slot, self.ones_s])
            lt = self.t32.next()
            P.add("act", lambda e: e.activation(out=lt.t[:, :], in_=self.ps[bsum][:, :], func=AF.Ln),
                  reads=[self.ps_s[bsum]], writes=[lt.slot])
            rs = self.rsr.next()
            P.add("act", lambda e: e.activation(out=rs.t[:, :], in_=lt.t[:, :], func=AF.Exp, scale=-1.0),
                  reads=[lt.slot], writes=[rs.slot])
            for oc in range(2):
                b = self.bank()
                f0 = (2 * hh + oc) * 128
                pairs = [(self.vv[:, mc, f0:f0 + 128], pT.t[:, mc, :]) for mc in range(2)]
                self.mm_group(b, pairs, TT, reads=[self.vv_s[0], self.vv_s[1], pT.slot])
                P.add("dve", lambda e, b=b, oc=oc: e.tensor_tensor(
                    out=self.oT[:, 2 * hh + oc, sl], in0=self.ps[b][:, :], in1=rs.t[:, :], op=ALU.mult),
                    reads=[self.ps_s[b], rs.slot], writes=[self.oT_s[2 * hh + oc][tt]])

        stage_a(0)
        for i in range(len(its)):
            if i + 1 < len(its):
                stage_a(i + 1)
            stage_b(i)
        self.proj_residual(self.xa_w["o"][l], self.oT, self.oT_s, KC, nxt=V_GFFN + l * 8)

    def ffn(self, l):
        P = self.P
        wg, wu, wd = self.ffn_w_gate[l], self.ffn_w_up[l], self.ffn_w_down[l]
        for s in range(6):
            cw = min(512, DFF - s * 512)
            sg, sg_s = self.ld(wg, s * 512, cw, KC)
            su, su_s = self.ld(wu, s * 512, cw, KC)
            for tt in range(NTT):
                self.want(("n", tt))
                sl = slice(tt * TT, (tt + 1) * TT)
                for fl in range(cw // 128):
                    f = s * 4 + fl
                    rk = (lambda k, sl=sl: self.nT[:, k, sl])
                    rs = [self.nT_s[k][tt] for k in range(KC)]
                    bg = self.linear(sg, sg_s, fl, KC, rk, rs, TT)
                    bu = self.linear(su, su_s, fl, KC, rk, rs, TT)
                    t = self.t32.next()
                    P.add("act", lambda e, t=t, bg=bg: e.activation(out=t.t[:, :], in_=self.ps[bg][:, :], func=AF.Silu),
                          reads=[self.ps_s[bg]], writes=[t.slot])
                    P.add("dve", lambda e, t=t, bu=bu, f=f, sl=sl: e.tensor_tensor(
                        out=self.hid[:, f, sl], in0=self.ps[bu][:, :], in1=t.t[:, :], op=ALU.mult),
                        reads=[self.ps_s[bu], t.slot], writes=[self.hid_s[f][tt]])
        if l == 0:
            return self.proj_residual(wd, self.hid, self.hid_s, FC, nxt=V_GMIX + 8)
        return self.proj_residual(wd, self.hid, self.hid_s, FC, nxt=V_GFIN, final=True)

    def phase_fence(self, old_slots, new_slots):
        ws = []
        rs = {}
        for s in old_slots:
            if s.w is not None:
                ws.append(s.w)
            for k, r in s.r.items():
                rs[(k, id(r))] = r
        for n in new_slots:
            for i, w in enumerate(ws):
                n.r[("fw", i)] = w
            for k, r in rs.items():
                n.r[("fr",) + k] = r

    def flat(self, *groups):
        out = []
        for g in groups:
            if isinstance(g, Slot):
                out.append(g)
            else:
                for x in g:
                    out.extend(self.flat(x))
        return out

    def build(self):
        nc = self.nc
        S_l0 = self.flat(self.a_ext_s, self.y_s, self.ac_s, self.acb_s, self.sqa_s, self.xh_s)
        S_l1 = self.flat(self.v_ln_s, self.vg4_s, self.y_s)
        S_xa = self.flat(self.qT_s, self.oT_s, [b.slot for b in self.pT.items], self.kT_s, self.vv_s)
        S_ff = self.flat(self.hid_s)
        S_fn = self.flat(self.nf_s, [b.slot for b in self.os.items])
        self.setup()
        prev = None
        st = None
        for h in range(NHALF):
            phases = [("l0", S_l0), ("xa0", S_xa), ("ff0", S_ff), ("l1", S_l1), ("xa1", S_xa), ("ff1", S_ff)]
            if st is None:
                self.load_x(h)
            else:
                self.load_x(h, st, lambda: self.phase_fence(S_fn, S_l0),
                            lambda j=jobs[1]: self.store_norm(j, drain=False))
            if h == 0:
                self.setup_dma()
                for i in range(2):
                    self.push("setup", lambda i=i: self.setup_mem(i), 6)
                for i in range(2):
                    self.push("setup", lambda i=i: self.setup_ws(i), 6)
            prev = S_l0
            jobs = None
            for name, slots in phases:
                if slots is not prev:
                    self.phase_fence(prev, slots)
                prev = slots
                if name == "l0":
                    self.l0_mixer()
                elif name == "l1":
                    self.l1_mixer()
                elif name.startswith("xa"):
                    self.xattn(int(name[2]))
                else:
                    if name == "ff1" and h + 1 < NHALF:
                        self.load_prefetch(h + 1)
                    jobs = self.ffn(int(name[2]))
            self.phase_fence(prev, S_fn)
            prev = S_fn
            self.store_norm(jobs[0])
            if h + 1 == NHALF:
                self.store_norm(jobs[1])
            st = self.store_pieces(h, act_tiles=(0, 1) if h + 1 < NHALF else (0,))
        for f in st:
            f()
        self.flush()
        assert not self.reserved, self.reserved
        dma_names = sorted(self.P.dma_cnt.keys())
        from contextlib import ExitStack
        with ExitStack() as st:
            neps = self.P.assign()
            esem = {(e, i): st.enter_context(nc.semaphore(f"s_{e}{i}")) for e in Prog.ENGS for i in range(neps[e])}
            dsem = {n: st.enter_context(nc.semaphore("d_" + n)) for n in dma_names}
            block = st.enter_context(nc.Block())
            self.P.emit(nc, block, esem, dsem)
        return nc


_CACHE = {}


def _get_program():
    if "nc" not in _CACHE:
        _CACHE["nc"] = Builder().build()
    return _CACHE["nc"]


def _fm(v):
    v = np.asarray(v, np.float32)
    return np.ascontiguousarray(v.reshape(-1, 128).T)


def _host_inputs(inp):
    f = lambda a: np.ascontiguousarray(np.asarray(a, np.float32))
    x = f(inp["x"])
    mem = f(inp["mem"])
    vec = np.zeros((128, NV), np.float32)
    for i in range(2):
        vec[:, V_GMIX + i * 8:V_GMIX + (i + 1) * 8] = _fm(inp["g_mix"][i])
        vec[:, V_GXA + i * 8:V_GXA + (i + 1) * 8] = _fm(inp["g_xattn"][i])
        vec[:, V_GMEM + i * 8:V_GMEM + (i + 1) * 8] = _fm(inp["g_mem"][i])
        vec[:, V_GFFN + i * 8:V_GFFN + (i + 1) * 8] = _fm(inp["g_ffn"][i])
    vec[:, V_GFIN:V_GFIN + 8] = _fm(inp["g_final"])
    caw = np.asarray(inp["ev_a_conv_w"], np.float32)[0]
    cbw = np.asarray(inp["ev_b_conv_w"], np.float32)[0]
    for j in range(4):
        vec[:, V_CAW + j * 31:V_CAW + (j + 1) * 31] = caw[:, j * 128:(j + 1) * 128].T
        vec[:, V_CBW + j * 3:V_CBW + (j + 1) * 3] = cbw[:, j * 128:(j + 1) * 128].T
    vec[:, V_CAB:V_CAB + 4] = _fm(inp["ev_a_conv_b"][0])
    vec[:, V_LAG:V_LAG + 4] = _fm(inp["ev_a_ln_g"][0])
    vec[:, V_LAB:V_LAB + 4] = _fm(inp["ev_a_ln_b"][0])
    vec[:, V_CBB:V_CBB + 4] = _fm(inp["ev_b_conv_b"][0])
    vec[:, V_LCG:V_LCG + 8] = _fm(inp["od_c_ln_g"][0])
    vec[:, V_LCB:V_LCB + 8] = _fm(inp["od_c_ln_b"][0])
    lnc = np.stack([f(inp["od_c_ln_g"])[0], f(inp["od_c_ln_b"])[0], f(inp["od_b_s"])[0].reshape(-1)], 0)
    shared = {
        "vecs": vec, "lnc": np.ascontiguousarray(lnc), "od_w_s": f(inp["od_w_s"])[0],
        "ev_w_in": f(inp["ev_w_in"])[0], "ev_w_out": f(inp["ev_w_out"])[0],
        "od_w_in": f(inp["od_w_in"])[0], "od_w_out": f(inp["od_w_out"])[0],
        "xa_w_q": f(inp["xa_w_q"]), "xa_w_k": f(inp["xa_w_k"]), "xa_w_v": f(inp["xa_w_v"]), "xa_w_o": f(inp["xa_w_o"]),
        "ffn_w_gate": f(inp["ffn_w_gate"]), "ffn_w_up": f(inp["ffn_w_up"]), "ffn_w_down": f(inp["ffn_w_down"]),
    }
    in_maps = []
    for c in range(NCORES):
        b, s0 = c // 4, (c % 4) * TCORE
        xin = np.zeros((NHALF, XROWS, D), np.float32)
        for h in range(NHALF):
            t0 = s0 + h * TH
            xin[h, :TH] = x[b, t0:t0 + TH]
            lo = max(t0 - HALO, 0)
            xin[h, TH + HALO - (t0 - lo):TH + HALO] = x[b, lo:t0]
            hi = min(t0 + TH + HALO, SEQ)
            xin[h, TH + HALO:TH + HALO + (hi - t0 - TH)] = x[b, t0 + TH:hi]
        m = dict(shared)
        m["x_in"] = xin
        m["mem"] = mem[b]
        in_maps.append(m)
    return in_maps


def kernel(**inputs):
    nc = _get_program()
    in_maps = _host_inputs(inputs)
    res = run_bass_kernel_spmd(nc, in_maps, core_ids=list(range(NCORES)))
    out = np.empty((2, SEQ, D), np.float32)
    for c in range(NCORES):
        b, s0 = c // 4, (c % 4) * TCORE
        out[b, s0:s0 + TCORE] = np.asarray(res.results[c]["out"])
    return out
```
